# Optimizing a Trainium2 kernel written in Bass

```python
import jax, jax.numpy as jnp
from jax import lax
import numpy as np

D_MODEL = 2048
BATCH = 4
SEQ = 2048
DEPTH = 2
DEC_BATCH = 128
DEC_SEQ = 8
PAST_LEN = 16384
PAGE_SIZE = 128

N_AB_LAYERS = (DEPTH + 1) // 2
N_C_LAYERS = DEPTH // 2
CONV_W = 4
HEAD_A = 64
D_INNER_A = D_MODEL
H_A = D_INNER_A // HEAD_A
G_A = 4
N_A = 128
CHUNK_A = 128
CONV_DIM_A = D_INNER_A + 2 * G_A * N_A
IN_A = D_INNER_A + CONV_DIM_A + H_A
HEAD_B = 64
D_B = D_MODEL
H_B = D_B // HEAD_B
R_W = 64
R_A = 64
R_G = 128
SHIFT_DIM_B = 3 * D_B + R_W + R_A + R_G
IN_B = SHIFT_DIM_B
IN_AB = IN_A + IN_B
MIX_AB = D_INNER_A + D_B
LRU_WIDTH = D_MODEL
H_C = 8
BLK_C = LRU_WIDTH // H_C
LRU_C = 8.0
D_FF = 5504
EPS = 1e-6
GN_EPS = 64e-5

kernel_name = 'hybrid_ssd_rwkv7_rglru_macaron_step'


def rmsnorm(x, g):
    xf = x.astype(jnp.float32)
    y = xf * lax.rsqrt(jnp.mean(xf * xf, axis=-1, keepdims=True) + EPS)
    return (y * g.astype(jnp.float32)).astype(x.dtype)


def swiglu(x, w_in, w_out):
    gate, up = jnp.split(x @ w_in, 2, axis=-1)
    return (jax.nn.silu(gate) * up) @ w_out


def causal_conv(u, buf, w, b):
    L = u.shape[1]
    full = jnp.concatenate([buf.astype(u.dtype), u], axis=1)
    y = b
    for k in range(CONV_W):
        y = y + full[:, k:k + L] * w[k]
    return y, full[:, L:]


def ssd_scan(xs, dt, a_head, bm, cm, h0):
    b, l = xs.shape[:2]
    q = min(CHUNK_A, l)
    nc = l // q
    e = H_A // G_A
    dtype = xs.dtype
    cum = jnp.cumsum((dt * a_head).reshape(b, nc, q, G_A, e), axis=2)
    xdt = (xs * dt[..., None].astype(dtype)).reshape(b, nc, q, G_A, e, HEAD_A)
    bm = bm.reshape(b, nc, q, G_A, N_A)
    cm = cm.reshape(b, nc, q, G_A, N_A)
    seg = cum[:, :, :, None] - cum[:, :, None, :]
    causal = jnp.tril(jnp.ones((q, q), bool))[None, None, :, :, None, None]
    decay_in = jnp.exp(jnp.where(causal, seg, -jnp.inf)).astype(dtype)
    cb = jnp.einsum('bcign,bcjgn->bcijg', cm, bm)
    y_diag = jnp.einsum('bcijge,bcjgep->bcigep', cb[..., None] * decay_in, xdt)
    tail = jnp.exp(cum[:, :, -1:] - cum).astype(dtype)
    st = jnp.einsum('bcjgn,bcjgep->bcgepn', bm, xdt * tail[..., None])
    chunk_decay = jnp.exp(cum[:, :, -1]).astype(dtype)

    def step(h, inp):
        dcy, s = inp
        return h * dcy[..., None, None] + s, h

    h_init = h0.astype(dtype).reshape(b, G_A, e, HEAD_A, N_A)
    h_last, h_starts = lax.scan(step, h_init, (jnp.moveaxis(chunk_decay, 1, 0), jnp.moveaxis(st, 1, 0)))
    h_starts = jnp.moveaxis(h_starts, 0, 1)
    y_off = jnp.einsum('bcign,bcgepn->bcigep', cm, h_starts) * jnp.exp(cum).astype(dtype)[..., None]
    y = (y_diag + y_off).reshape(b, l, H_A, HEAD_A)
    return y, h_last.reshape(b, H_A, HEAD_A, N_A)


def mamba2_mixer(p_a, conv_buf, h0, conv_w, conv_b, dt_bias, a_log, d_skip, gnorm):
    b, l, _ = p_a.shape
    z, xbc, dt_raw = jnp.split(p_a, [D_INNER_A, D_INNER_A + CONV_DIM_A], axis=-1)
    xbc, new_buf = causal_conv(xbc, conv_buf, conv_w, conv_b)
    xbc = jax.nn.silu(xbc)
    xs, bm, cm = jnp.split(xbc, [D_INNER_A, D_INNER_A + G_A * N_A], axis=-1)
    dt = jax.nn.softplus((dt_raw + dt_bias).astype(jnp.float32))
    a_head = -jnp.exp(a_log.astype(jnp.float32))
    xs = xs.reshape(b, l, H_A, HEAD_A)
    y, h_last = ssd_scan(xs, dt, a_head, bm.reshape(b, l, G_A, N_A), cm.reshape(b, l, G_A, N_A), h0)
    y = y + xs * d_skip[:, None]
    y = y.reshape(b, l, D_INNER_A) * jax.nn.silu(z)
    yg = y.reshape(b, l, G_A, D_INNER_A // G_A).astype(jnp.float32)
    yg = yg * lax.rsqrt(jnp.mean(yg * yg, axis=-1, keepdims=True) + EPS)
    y = (yg.reshape(b, l, D_INNER_A) * gnorm.astype(jnp.float32)).astype(p_a.dtype)
    return y, new_buf, h_last


def wkv7_scan(r, decay, k, v, kk, a, s0):
    def step(S, inp):
        r_t, w_t, k_t, v_t, kk_t, a_t = inp
        sa = jnp.einsum('bhvk,bhk->bhv', S, kk_t)
        S = S * w_t[:, :, None, :] - sa[..., None] * (kk_t * a_t)[:, :, None, :] + v_t[..., None] * k_t[:, :, None, :]
        return S, jnp.einsum('bhvk,bhk->bhv', S, r_t)

    xs = (jnp.moveaxis(r, 1, 0), jnp.moveaxis(decay, 1, 0), jnp.moveaxis(k, 1, 0),
          jnp.moveaxis(v, 1, 0), jnp.moveaxis(kk, 1, 0), jnp.moveaxis(a, 1, 0))
    s_last, o = lax.scan(step, s0.astype(r.dtype), xs)
    return jnp.moveaxis(o, 0, 1), s_last


def rwkv7_mixer(p_b, shift_buf, s0, mu, w0, w2, a0, a2, g2, k_k, k_a, r_k, ln_w, ln_b):
    b, l, _ = p_b.shape
    dtype = p_b.dtype
    prev = jnp.concatenate([shift_buf.astype(dtype), p_b[:, :-1]], axis=1)
    ps = p_b + mu * (prev - p_b)
    new_buf = p_b[:, -1:]
    r, k, v, xw, xa, xg = jnp.split(ps, [D_B, 2 * D_B, 3 * D_B, 3 * D_B + R_W, 3 * D_B + R_W + R_A], axis=-1)
    wlog = -jax.nn.softplus(-(w0 + jnp.tanh(xw) @ w2).astype(jnp.float32)) - 0.5
    decay = jnp.exp(-jnp.exp(wlog)).astype(dtype)
    a = jax.nn.sigmoid(a0 + xa @ a2)
    g = jax.nn.sigmoid(xg) @ g2
    kkf = (k * k_k).reshape(b, l, H_B, HEAD_B).astype(jnp.float32)
    kk = (kkf / jnp.maximum(jnp.sqrt(jnp.sum(kkf * kkf, axis=-1, keepdims=True)), 1e-12)).astype(dtype)
    k = k * (1 + (a - 1) * k_a)
    sh = (b, l, H_B, HEAD_B)
    r, k, v, a, decay = r.reshape(sh), k.reshape(sh), v.reshape(sh), a.reshape(sh), decay.reshape(sh)
    o, s_last = wkv7_scan(r, decay, k, v, kk, a, s0)
    of = o.astype(jnp.float32)
    mean = jnp.mean(of, axis=-1, keepdims=True)
    var = jnp.mean(jnp.square(of - mean), axis=-1, keepdims=True)
    on = ((of - mean) * lax.rsqrt(var + GN_EPS)).reshape(b, l, D_B)
    on = (on * ln_w.astype(jnp.float32) + ln_b.astype(jnp.float32)).astype(dtype)
    bonus = (jnp.sum(r * k * r_k, axis=-1, keepdims=True) * v).reshape(b, l, D_B)
    return (on + bonus) * g, new_buf, s_last


def linear_scan(a, bterm, h0):
    bterm = bterm.at[:, 0].add(a[:, 0] * h0)

    def comb(lhs, rhs):
        a1, b1 = lhs
        a2, b2 = rhs
        return a1 * a2, a2 * b1 + b2

    _, h = lax.associative_scan(comb, (a, bterm), axis=1)
    return h


def rglru_mixer(p_c, conv_buf, h0, conv_w, conv_b, w_ga, b_ga, w_gx, b_gx, lam):
    b, l, _ = p_c.shape
    gate_branch, xr = jnp.split(p_c, 2, axis=-1)
    xc, new_buf = causal_conv(xr, conv_buf, conv_w, conv_b)
    xh = xc.reshape(b, l, H_C, BLK_C)
    rg = jax.nn.sigmoid(jnp.einsum('blhi,hij->blhj', xh, w_ga).reshape(b, l, LRU_WIDTH) + b_ga)
    ig = jax.nn.sigmoid(jnp.einsum('blhi,hij->blhj', xh, w_gx).reshape(b, l, LRU_WIDTH) + b_gx)
    log_a = -LRU_C * rg.astype(jnp.float32) * jax.nn.softplus(-lam.astype(jnp.float32))
    a = jnp.exp(log_a)
    mult = jnp.sqrt(-jnp.expm1(2.0 * log_a))
    bterm = mult * (ig * xc).astype(jnp.float32)
    h = linear_scan(a, bterm, h0.astype(jnp.float32))
    y = h.astype(p_c.dtype) * jax.nn.gelu(gate_branch)
    return y, new_buf, h[:, -1].astype(p_c.dtype)


def setup_inputs(seed: int = 0) -> dict:
    key = jax.random.key(seed)
    ks = iter(jax.random.split(key, 64))
    f32 = jnp.float32

    def nrm(shape, scale):
        return jax.random.normal(next(ks), shape, f32) * scale

    def uni(shape, lo, hi):
        return jax.random.uniform(next(ks), shape, f32, lo, hi)

    dt0 = jnp.exp(uni((N_AB_LAYERS, H_A), float(np.log(1e-3)), float(np.log(1e-1))))
    a_tgt = uni((N_C_LAYERS, LRU_WIDTH), 0.9, 0.999)
    s_tgt = a_tgt ** (1.0 / LRU_C)
    return {
        'x_prompt': nrm((BATCH, SEQ, D_MODEL), 1.0),
        'x_sample': nrm((DEC_BATCH, DEC_SEQ, D_MODEL), 1.0),
        'state_ssm_a': nrm((N_AB_LAYERS, DEC_BATCH, H_A, HEAD_A, N_A), 0.1),
        'state_conv_a': nrm((N_AB_LAYERS, DEC_BATCH, CONV_W - 1, CONV_DIM_A), 1.0),
        'state_wkv_b': nrm((N_AB_LAYERS, DEC_BATCH, H_B, HEAD_B, HEAD_B), 1.0),
        'state_shift_b': nrm((N_AB_LAYERS, DEC_BATCH, 1, SHIFT_DIM_B), 1.0),
        'state_lru_c': nrm((N_C_LAYERS, DEC_BATCH, LRU_WIDTH), 0.5),
        'state_conv_c': nrm((N_C_LAYERS, DEC_BATCH, CONV_W - 1, LRU_WIDTH), 1.0),
        'norm_gain': 1.0 + nrm((DEPTH, 3, D_MODEL), 0.02),
        'w_ffn_in': nrm((DEPTH, 2, D_MODEL, 2 * D_FF), D_MODEL ** -0.5),
        'w_ffn_out': nrm((DEPTH, 2, D_FF, D_MODEL), D_FF ** -0.5),
        'w_in_ab': nrm((N_AB_LAYERS, D_MODEL, IN_AB), D_MODEL ** -0.5),
        'conv_w_a': nrm((N_AB_LAYERS, CONV_W, CONV_DIM_A), CONV_W ** -0.5),
        'conv_b_a': nrm((N_AB_LAYERS, CONV_DIM_A), 0.02),
        'dt_bias_a': dt0 + jnp.log(-jnp.expm1(-dt0)),
        'a_log_a': jnp.log(uni((N_AB_LAYERS, H_A), 1.0, 16.0)),
        'd_skip_a': 1.0 + nrm((N_AB_LAYERS, H_A), 0.1),
        'gnorm_a': 1.0 + nrm((N_AB_LAYERS, D_INNER_A), 0.02),
        'mu_b': uni((N_AB_LAYERS, SHIFT_DIM_B), 0.0, 1.0),
        'w0_b': uni((N_AB_LAYERS, D_B), -6.0, -1.0),
        'w2_b': nrm((N_AB_LAYERS, R_W, D_B), 0.1 * R_W ** -0.5),
        'a0_b': nrm((N_AB_LAYERS, D_B), 0.1),
        'a2_b': nrm((N_AB_LAYERS, R_A, D_B), 0.1 * R_A ** -0.5),
        'g2_b': nrm((N_AB_LAYERS, R_G, D_B), R_G ** -0.5),
        'k_k_b': 0.85 + nrm((N_AB_LAYERS, D_B), 0.02),
        'k_a_b': 1.0 + nrm((N_AB_LAYERS, D_B), 0.02),
        'r_k_b': nrm((N_AB_LAYERS, H_B, HEAD_B), 0.1),
        'ln_w_b': 1.0 + nrm((N_AB_LAYERS, D_B), 0.02),
        'ln_b_b': nrm((N_AB_LAYERS, D_B), 0.02),
        'w_out_ab': nrm((N_AB_LAYERS, MIX_AB, D_MODEL), MIX_AB ** -0.5),
        'w_in_c': nrm((N_C_LAYERS, D_MODEL, 2 * LRU_WIDTH), D_MODEL ** -0.5),
        'conv_w_c': nrm((N_C_LAYERS, CONV_W, LRU_WIDTH), CONV_W ** -0.5),
        'conv_b_c': nrm((N_C_LAYERS, LRU_WIDTH), 0.02),
        'w_gate_a_c': nrm((N_C_LAYERS, H_C, BLK_C, BLK_C), BLK_C ** -0.5),
        'b_gate_a_c': nrm((N_C_LAYERS, LRU_WIDTH), 0.02),
        'w_gate_x_c': nrm((N_C_LAYERS, H_C, BLK_C, BLK_C), BLK_C ** -0.5),
        'b_gate_x_c': nrm((N_C_LAYERS, LRU_WIDTH), 0.02),
        'lambda_c': jnp.log(s_tgt) - jnp.log1p(-s_tgt),
        'w_out_c': nrm((N_C_LAYERS, LRU_WIDTH, D_MODEL), LRU_WIDTH ** -0.5),
        'final_norm_gain': 1.0 + nrm((D_MODEL,), 0.02),
    }


def reference(x_prompt, x_sample, state_ssm_a, state_conv_a, state_wkv_b, state_shift_b, state_lru_c, state_conv_c,
              norm_gain, w_ffn_in, w_ffn_out, w_in_ab, conv_w_a, conv_b_a, dt_bias_a, a_log_a, d_skip_a, gnorm_a,
              mu_b, w0_b, w2_b, a0_b, a2_b, g2_b, k_k_b, k_a_b, r_k_b, ln_w_b, ln_b_b, w_out_ab,
              w_in_c, conv_w_c, conv_b_c, w_gate_a_c, b_gate_a_c, w_gate_x_c, b_gate_x_c, lambda_c, w_out_c,
              final_norm_gain):
    def trunk(x, ssm0, conva0, wkv0, shift0, lru0, convc0):
        ssm_n, conva_n, wkv_n, shift_n, lru_n, convc_n = [], [], [], [], [], []
        for i in range(DEPTH):
            j = i // 2
            x = x + 0.5 * swiglu(rmsnorm(x, norm_gain[i, 0]), w_ffn_in[i, 0], w_ffn_out[i, 0])
            h = rmsnorm(x, norm_gain[i, 1])
            if i % 2 == 0:
                p = h @ w_in_ab[j]
                ya, buf_a, hs = mamba2_mixer(p[..., :IN_A], conva0[j], ssm0[j], conv_w_a[j], conv_b_a[j],
                                             dt_bias_a[j], a_log_a[j], d_skip_a[j], gnorm_a[j])
                yb, buf_b, sb = rwkv7_mixer(p[..., IN_A:], shift0[j], wkv0[j], mu_b[j], w0_b[j], w2_b[j],
                                            a0_b[j], a2_b[j], g2_b[j], k_k_b[j], k_a_b[j], r_k_b[j],
                                            ln_w_b[j], ln_b_b[j])
                x = x + jnp.concatenate([ya, yb], axis=-1) @ w_out_ab[j]
                ssm_n.append(hs)
                conva_n.append(buf_a)
                wkv_n.append(sb)
                shift_n.append(buf_b)
            else:
                p = h @ w_in_c[j]
                yc, buf_c, hc = rglru_mixer(p, convc0[j], lru0[j], conv_w_c[j], conv_b_c[j], w_gate_a_c[j],
                                            b_gate_a_c[j], w_gate_x_c[j], b_gate_x_c[j], lambda_c[j])
                x = x + yc @ w_out_c[j]
                lru_n.append(hc)
                convc_n.append(buf_c)
            x = x + 0.5 * swiglu(rmsnorm(x, norm_gain[i, 2]), w_ffn_in[i, 1], w_ffn_out[i, 1])
        y = rmsnorm(x, final_norm_gain)
        return (y, jnp.stack(ssm_n), jnp.stack(conva_n), jnp.stack(wkv_n), jnp.stack(shift_n),
                jnp.stack(lru_n), jnp.stack(convc_n))

    bp = x_prompt.shape[0]
    dtp = x_prompt.dtype

    def zeros_like_state(s):
        return jnp.zeros((s.shape[0], bp) + s.shape[2:], dtp)

    y_prompt, ssm_p, conva_p, wkv_p, shift_p, lru_p, convc_p = trunk(
        x_prompt, zeros_like_state(state_ssm_a), zeros_like_state(state_conv_a), zeros_like_state(state_wkv_b),
        zeros_like_state(state_shift_b), zeros_like_state(state_lru_c), zeros_like_state(state_conv_c))
    y_sample, ssm_s, conva_s, wkv_s, shift_s, lru_s, convc_s = trunk(
        x_sample, state_ssm_a, state_conv_a, state_wkv_b, state_shift_b, state_lru_c, state_conv_c)
    return (y_prompt, y_sample, ssm_p, conva_p, wkv_p, shift_p, lru_p, convc_p,
            ssm_s, conva_s, wkv_s, shift_s, lru_s, convc_s)
```

```python
import numpy as np
from contextlib import ExitStack
import concourse.bass as bass
import concourse.mybir as mybir
from concourse.bass_utils import run_bass_kernel_spmd

F32 = mybir.dt.float32
F32R = mybir.dt.float32r
AF = mybir.ActivationFunctionType
ALU = mybir.AluOpType
AX = mybir.AxisListType
MUL, ADD, SUB, MAX, MIN = ALU.mult, ALU.add, ALU.subtract, ALU.max, ALU.min

D = 2048
KD = 16
DFF = 5504
NF = 43
SEQ = 2048
NSEG = 16
LSEG = 8
EPS = 1e-6
GN_EPS = 64e-5
NSLOT = 3
SLOT = 4096
NDS = 12
IN_AB = 11552
COL_Z, COL_XS, COL_B, COL_C, COL_DT = 0, 2048, 4096, 4608, 5120
COL_R, COL_K, COL_V, COL_LW, COL_XG = 5152, 7200, 9248, 11296, 11424
NTP = 20
NTQ = 21
TPW = 260


class Tok:
    __slots__ = ("sem", "val")

    def __init__(self, sem, val):
        self.sem = sem
        self.val = val


class Region:
    __slots__ = ("name", "w", "r")

    def __init__(self, name):
        self.name = name
        self.w = None
        self.r = {}


class _FakeSem:
    pass


class KB:
    def __init__(self, nc, es, mode, sched, need_inc):
        self.nc = nc
        self.mode = mode
        self.dry = mode == "collect"
        self.real = mode == "real"
        self.sched = sched
        self.need_inc = need_inc if need_inc is not None else set()
        self.req = []
        self.eng = {"pe": nc.tensor, "act": nc.scalar, "dve": nc.vector, "pool": nc.gpsimd, "sp": nc.sync}
        self.sem = {}
        self.semname = {}
        self.cnt = {}
        self.rank = {}
        self.last = {}
        self.waited = {}
        self.dsem = {}
        self.dn = {}
        self.dhist = {}
        self.regs = {}
        self.all_dma = []
        self.n_ins = 0
        if mode != "collect":
            for e in self.eng:
                self.sem[e] = es.enter_context(nc.semaphore("s_" + e)) if self.real else _FakeSem()
                self.semname[id(self.sem[e])] = e
                self.cnt[e] = 0
                self.rank[e] = 0
                self.last[e] = None
            for q in ("sp", "pool"):
                self.dsem[q] = [es.enter_context(nc.semaphore("d_%s_%d" % (q, i))) if self.real else _FakeSem() for i in range(NDS)]
                self.dn[q] = 0
                self.dhist[q] = [None] * NDS
        self.w_issued = 0
        self.w_next = 0
        self.wslots = None

    def R(self, *key):
        r = self.regs.get(key)
        if r is None:
            r = Region(key)
            self.regs[key] = r
        return r

    def _need(self, e, tok, acc):
        if tok is None:
            return
        if e == "pe" and tok.sem is self.sem["pe"]:
            return
        key = (e, id(tok.sem))
        if self.waited.get(key, 0) >= tok.val:
            return
        self.waited[key] = tok.val
        acc[id(tok.sem)] = tok
        if self.mode == "sim":
            en = self.semname.get(id(tok.sem))
            if en is not None:
                self.need_inc.add((en, tok.val))

    def _emit_wait(self, e, t):
        if self.real:
            self.eng[e].wait_ge(t.sem, t.val)
        self.n_ins += 1

    def _wait(self, e, tok):
        acc = {}
        self._need(e, tok, acc)
        for t in acc.values():
            self._emit_wait(e, t)

    def _collect(self, e, reads, writes):
        acc = {}
        for R in reads:
            self._need(e, R.w, acc)
        for R in writes:
            self._need(e, R.w, acc)
            for t in R.r.values():
                self._need(e, t, acc)
        return list(acc.values())

    def _commit(self, tok, reads, writes):
        for R in reads:
            R.r[id(tok.sem)] = tok
        for R in writes:
            R.w = tok
            R.r = {}

    def op(self, e, fn, reads=(), writes=()):
        if self.dry:
            return None
        toks = self._collect(e, reads, writes)
        for t in toks[:-1]:
            self._emit_wait(e, t)
        self.cnt[e] += 1
        idx = self.cnt[e]
        if self.real:
            ins = fn(self.eng[e])
            if toks:
                ins._wait_ge(toks[-1].sem, toks[-1].val)
            if (e, idx) in self.need_inc:
                self.rank[e] += 1
                ins.then_inc(self.sem[e], 1)
                tok = Tok(self.sem[e], self.rank[e])
            else:
                tok = Tok(self.sem[e], self.rank[e])
        else:
            tok = Tok(self.sem[e], idx)
        self.n_ins += 1
        self.last[e] = (idx, tok)
        self._commit(tok, reads, writes)
        return tok

    def dma(self, q, out, in_, reads=(), writes=(), r32=False, slow=False):
        if self.dry:
            return None
        for t in self._collect(q, reads, writes):
            self._emit_wait(q, t)
        n = self.dn[q]
        slot = n % NDS
        self._wait(q, self.dhist[q][slot])
        sem = self.dsem[q][slot]
        tok = Tok(sem, 16 * (n // NDS + 1))
        if self.real:
            if r32:
                self.nc.dge_precook = False
                ins = self.eng[q].dma_start(out=out.bitcast(F32R), in_=in_.bitcast(F32R))
                self.nc.dge_precook = True
            elif slow:
                ins = self.eng[q].dma_start(out=out, in_=in_, allow_slow_non_contiguous=True)
            else:
                ins = self.eng[q].dma_start(out=out, in_=in_)
            ins.then_inc(sem, 16)
        self.n_ins += 1
        self.dn[q] = n + 1
        self.dhist[q][slot] = tok
        self._commit(tok, reads, writes)
        self.all_dma.append(tok)
        return tok

    def barrier(self):
        if self.dry:
            return
        ces = ("pe", "act", "dve", "pool")
        for e in ces:
            for f in ces:
                if f != e and self.last[f] is not None:
                    idx, tok = self.last[f]
                    if self.real and (f, idx) not in self.need_inc:
                        continue
                    self._wait(e, tok)
            for t in self.dhist["pool"]:
                self._wait(e, t)

    def finish(self):
        if self.dry:
            return
        for tok in self.all_dma:
            self._wait("sp", tok)

    def wnext(self, spec):
        i = self.w_next
        self.w_next += 1
        s = i % NSLOT
        if self.dry:
            self.req.append(spec)
            return self.wslots[s], self.R("wslot", s)
        assert self.sched[i] == spec, (i, self.sched[i], spec)
        hi = min(i + NSLOT, len(self.sched))
        while self.w_issued < hi:
            j = self.w_issued
            sj = j % NSLOT
            for (dst, src) in self.wsrc(self.sched[j], self.wslots[sj]):
                self.dma("sp", dst, src, writes=[self.R("wslot", sj)], r32=True)
            self.w_issued += 1
        return self.wslots[s], self.R("wslot", s)


def _r(ap):
    return ap.bitcast(F32R)


def build(cfg):
    sched = None
    need = None
    for mode in ("collect", "sim", "real"):
        nc = bass.Bass("TRN2", target_bir_lowering=False)
        with ExitStack() as es:
            kb = KB(nc, es, mode, sched, need)
            _program(nc, es, kb, cfg)
        if mode == "collect":
            sched = kb.req
        elif mode == "sim":
            need = kb.need_inc
    return nc, kb


def _program(nc, es, kb, cfg):
    stages = cfg["stages"]
    use_ffn = any(s.startswith("ffn") for s in stages)
    use_ab = "mixab" in stages
    use_c = "mixc" in stages

    def din(name, shape):
        return nc.dram_tensor(name, list(shape), F32, kind="ExternalInput").ap()

    def dout(name, shape):
        return nc.dram_tensor(name, list(shape), F32, kind="ExternalOutput").ap()

    xp = din("xp", [SEQ, D])
    xs = din("xs", [128, D])
    norm_gain = din("norm_gain", [2, 3, D])
    final_gain = din("final_norm_gain", [D])
    yp = dout("yp", [SEQ, D])
    ys = dout("ys", [128, D])
    if use_ffn:
        w_ffn_in = din("w_ffn_in", [2, 2, D, 2 * DFF])
        w_ffn_out = din("w_ffn_out", [2, 2, DFF, D])
    if use_c:
        w_in_c = din("w_in_c", [D, 2 * D])
        w_out_c = din("w_out_c", [D, D])
        conv_w_c = din("conv_w_c", [4, D])
        conv_b_c = din("conv_b_c", [D])
        w_ga = din("w_gate_a_c", [8, 256, 256])
        w_gx = din("w_gate_x_c", [8, 256, 256])
        b_ga = din("b_gate_a_c", [D])
        b_gx = din("b_gate_x_c", [D])
        lam = din("lambda_c", [D])
        st_lru = din("st_lru", [NSEG, D])
        st_convc = din("st_convc", [NSEG * 3, D])
        lru_p = dout("lru_p", [D])
        convc_p = dout("convc_p", [3, D])
        lru_s = dout("lru_s", [NSEG, D])
        convc_s = dout("convc_s", [NSEG * 3, D])
    if use_ab:
        w_in_ab = din("w_in_ab", [D, IN_AB])
        w_out_ab = din("w_out_ab", [2 * D, D])
        conv_w_a = din("conv_w_a", [4, 3072])
        conv_b_a = din("conv_b_a", [3072])
        dt_bias = din("dt_bias_a", [32])
        a_log = din("a_log_a", [32])
        d_skip = din("d_skip_a", [32])
        gnorm = din("gnorm_a", [D])
        mu_b = din("mu_b", [6400])
        w0_b = din("w0_b", [D])
        w2_b = din("w2_b", [64, D])
        a0_b = din("a0_b", [D])
        a2_b = din("a2_b", [64, D])
        g2_b = din("g2_b", [128, D])
        k_k_b = din("k_k_b", [D])
        k_a_b = din("k_a_b", [D])
        r_k_b = din("r_k_b", [D])
        ln_w_b = din("ln_w_b", [D])
        ln_b_b = din("ln_b_b", [D])
        st_ssm = din("st_ssm", [NSEG, 32, 64, 128])
        st_conva = din("st_conva", [NSEG * 3, 3072])
        st_wkv = din("st_wkv", [NSEG, 32, 64, 64])
        st_shift = din("st_shift", [NSEG, 6400])
        ssm_p = dout("ssm_p", [32, 64, 128])
        conva_p = dout("conva_p", [3, 3072])
        wkv_p = dout("wkv_p", [32, 64, 64])
        shift_p = dout("shift_p", [6400])
        ssm_s = dout("ssm_s", [NSEG, 32, 64, 128])
        conva_s = dout("conva_s", [NSEG * 3, 3072])
        wkv_s = dout("wkv_s", [NSEG, 32, 64, 64])
        shift_s = dout("shift_s", [NSEG, 6400])

    def sb(name, shape, dt=F32):
        return es.enter_context(nc.sbuf_tensor(name, list(shape), dt))

    TT = 256
    x = sb("x", [128, KD, TT])
    hT = sb("hT", [128, KD, TT])
    hid = sb("hid", [128, 28, TT])
    hidf = hid[:].rearrange("p a b -> p (a b)")
    scr = sb("scr", [128, 5952])
    kb.wslots = [sb("wslot%d" % i, [128, SLOT]) for i in range(NSLOT)]
    ident = sb("ident", [128, 128])
    ones = sb("ones", [128, 128])
    bones = sb("bones", [128, 128])
    params = sb("params", [128, 6 * 128])
    tp = [sb("tp%d" % i, [128, TPW]) for i in range(NTP)]
    tq = [sb("tq%d" % i, [128, TPW]) for i in range(NTQ)]
    ps = [es.enter_context(nc.psum_tensor("ps%d" % i, [128, 512], F32)) for i in range(8)]
    m_sl = [sb("m_sl%d" % i, [128, 128]) for i in range(2)]
    m_iu = [sb("m_iu%d" % i, [128, 128]) for i in range(2)]
    m_neg = [sb("m_neg%d" % i, [128, 128]) for i in range(2)]
    m_pair = [sb("m_pair%d" % i, [128, 256]) for i in range(2)]
    m_reset = [sb("m_reset%d" % i, [128, 256]) for i in range(2)]
    segmask = sb("segmask", [128, 16])
    bsel = sb("bsel", [16, 128])
    es.enter_context(nc.Block())

    R = kb.R
    op = kb.op

    def MM(out, lhsT, rhs, start, stop, rd, wr):
        op("pe", lambda e: e.matmul(out, lhsT, rhs, start=start, stop=stop), reads=rd, writes=wr)

    def TR(out, in_, rd, wr):
        p = in_.shape[0]
        op("pe", lambda e: e.transpose(out, in_, ident[:p, :p]), reads=list(rd) + [R("ident")], writes=wr)

    def TTo(e, out, in0, in1, o, rd, wr):
        op(e, lambda g: g.tensor_tensor(out=out, in0=in0, in1=in1, op=o), reads=rd, writes=wr)

    def TS(e, out, in0, s1, s2, o0, o1, rd, wr):
        if o1 is None:
            op(e, lambda g: g.tensor_scalar(out=out, in0=in0, scalar1=s1, scalar2=None, op0=o0), reads=rd, writes=wr)
        else:
            op(e, lambda g: g.tensor_scalar(out=out, in0=in0, scalar1=s1, scalar2=s2, op0=o0, op1=o1), reads=rd, writes=wr)

    def STT(out, in0, scalar, in1, o0, o1, rd, wr):
        op("dve", lambda g: g.scalar_tensor_tensor(out=out, in0=in0, scalar=scalar, in1=in1, op0=o0, op1=o1), reads=rd, writes=wr)

    def ACT(out, in_, func, rd, wr, bias=None, scale=None):
        kw = {}
        if bias is not None:
            kw["bias"] = bias
        if scale is not None:
            kw["scale"] = scale
        op("act", lambda g: g.activation(out=out, in_=in_, func=func, **kw), reads=rd, writes=wr)

    def CP(e, out, in_, rd, wr):
        if e == "act":
            op("act", lambda g: g.activation(out=out, in_=in_, func=AF.Copy), reads=rd, writes=wr)
        else:
            op(e, lambda g: g.tensor_copy(out=out, in_=in_), reads=rd, writes=wr)

    def MS(e, ap, val, wr):
        op(e, lambda g: g.memset(ap, val), writes=wr)

    def v3(ap, n):
        return ap.rearrange("p (s l) -> p s l", s=n)

    pctr = [0]

    def PS(lo=0, hi=6):
        i = lo + pctr[0] % (hi - lo)
        pctr[0] += 1
        return ps[i], R("ps", i)

    def T_(i):
        return tp[i], R("tp", i)

    def Q_(i):
        return tq[i], R("tq", i)

    def wsrc(spec, slot):
        kind = spec[0]
        if kind == "ffn_in":
            _, li, fi, j = spec
            W = w_ffn_in[li, fi].rearrange("(k p) c -> p k c", p=128)
            dst = slot[:].rearrange("p (g k c) -> p g k c", g=2, k=KD)
            return [(dst[:, g], W[:, :, g * DFF + j * 128: g * DFF + (j + 1) * 128]) for g in range(2)]
        if kind == "ffn_out":
            _, li, fi, j0, j1, m = spec
            W = w_ffn_out[li, fi].rearrange("(j p) c -> p j c", p=128)
            nf = j1 - j0
            dst = slot[:, :nf * 128].rearrange("p (j c) -> p j c", j=nf)
            return [(dst, W[:, j0:j1, m * 128:(m + 1) * 128])]
        if kind == "cols":
            _, wname, row0, pieces = spec
            Wd = {"w_in_c": w_in_c if use_c else None, "w_out_c": w_out_c if use_c else None,
                  "w_in_ab": w_in_ab if use_ab else None, "w_out_ab": w_out_ab if use_ab else None}[wname]
            W = Wd[row0:row0 + D, :].rearrange("(k p) c -> p k c", p=128)
            tot = sum(n for (_, n) in pieces)
            dst = slot[:, :KD * tot].rearrange("p (k c) -> p k c", k=KD)
            out = []
            o = 0
            for (c0, n) in pieces:
                out.append((dst[:, :, o:o + n], W[:, :, c0:c0 + n]))
                o += n
            return out
        if kind == "gates":
            _, h = spec
            dst = slot[:, :1024].rearrange("p (g i c) -> p g i c", g=2, i=2)
            return [(dst[:, 0], w_ga[h].rearrange("(i p) c -> p i c", p=128)),
                    (dst[:, 1], w_gx[h].rearrange("(i p) c -> p i c", p=128))]
        if kind == "lora":
            _, hh = spec
            W = w_in_ab.rearrange("(k p) c -> p k c", p=128)
            dst = slot[:, :KD * 128].rearrange("p (k c) -> p k c", k=KD)
            o = KD * 128
            return [(dst, W[:, :, COL_V + hh * 128: COL_V + (hh + 1) * 128]),
                    (slot[0:64, o:o + 128], w2_b[:, hh * 128:(hh + 1) * 128]),
                    (slot[64:128, o:o + 128], a2_b[:, hh * 128:(hh + 1) * 128]),
                    (slot[:, o + 128:o + 256], g2_b[:, hh * 128:(hh + 1) * 128])]
        if kind == "wdt":
            W = w_in_ab.rearrange("(k p) c -> p k c", p=128)
            dst = slot[:, :KD * 32].rearrange("p (k c) -> p k c", k=KD)
            return [(dst, W[:, :, COL_DT:COL_DT + 32])]
        raise ValueError(spec)

    kb.wsrc = wsrc

    MS("pool", ones[:], 1.0, [R("ones")])
    MS("pool", ident[:], 1.0, [R("ident")])
    op("pool", lambda e: e.affine_select(out=_r(ident[:]), in_=ident[:], pattern=[[-1, 128]],
                                         compare_op=ALU.is_equal, fill=0.0, base=0, channel_multiplier=1),
       reads=[R("ident")], writes=[R("ident")])
    MS("pool", bones[:], 0.0, [R("bones")])
    MS("pool", bones[0:64, 0:64], 1.0, [R("bones")])
    MS("pool", bones[64:128, 64:128], 1.0, [R("bones")])
    MS("pool", bsel[:], 1.0, [R("bsel")])
    op("pool", lambda e: e.affine_select(out=bsel[:], in_=bsel[:], pattern=[[1, 128]], compare_op=ALU.is_ge, fill=0.0,
                                         base=0, channel_multiplier=-8), reads=[R("bsel")], writes=[R("bsel")])
    op("pool", lambda e: e.affine_select(out=bsel[:], in_=bsel[:], pattern=[[-1, 128]], compare_op=ALU.is_ge, fill=0.0,
                                         base=7, channel_multiplier=8), reads=[R("bsel")], writes=[R("bsel")])
    MS("pool", m_sl[0][:], 1.0, [R("m_sl", 0)])
    op("pool", lambda e: e.affine_select(out=m_sl[0][:], in_=m_sl[0][:], pattern=[[-1, 128]], compare_op=ALU.is_gt, fill=0.0,
                                         base=0, channel_multiplier=1), reads=[R("m_sl", 0)], writes=[R("m_sl", 0)])
    MS("pool", m_iu[0][:], 1.0, [R("m_iu", 0)])
    op("pool", lambda e: e.affine_select(out=m_iu[0][:], in_=m_iu[0][:], pattern=[[1, 128]], compare_op=ALU.is_ge, fill=0.0,
                                         base=0, channel_multiplier=-1), reads=[R("m_iu", 0)], writes=[R("m_iu", 0)])
    pseg, rseg = ps[6], R("ps", 6)
    MM(pseg[:, 0:128], bsel[:], bsel[:], True, True, [R("bsel")], [rseg])
    TTo("dve", m_sl[1][:], m_sl[0][:], pseg[:, 0:128], MUL, [R("m_sl", 0), rseg], [R("m_sl", 1)])
    TTo("dve", m_iu[1][:], m_iu[0][:], pseg[:, 0:128], MUL, [R("m_iu", 0), rseg], [R("m_iu", 1)])
    TR(ps[7][:, 0:16], bsel[:], [R("bsel")], [R("ps", 7)])
    CP("dve", segmask[:], ps[7][:, 0:16], [R("ps", 7)], [R("segmask")])
    for i in range(2):
        TS("dve", m_neg[i][:], m_iu[i][:], 30000.0, -30000.0, MUL, ADD, [R("m_iu", i)], [R("m_neg", i)])
        CP("dve", _r(m_pair[i][:, 128:256]), m_iu[i][:], [R("m_iu", i)], [R("m_pair", i)])
        TTo("dve", _r(m_pair[i][:, 0:128]), m_iu[i][:], ident[:], SUB, [R("m_iu", i), R("ident")], [R("m_pair", i)])
        MS("dve", m_reset[i][:], 1.0, [R("m_reset", i)])
    MS("dve", m_reset[0][:].rearrange("p (c t) -> p c t", t=128)[:, :, 0:1], 0.0, [R("m_reset", 0)])
    MS("dve", m_reset[1][:].rearrange("p (c t) -> p c t", t=8)[:, :, 0:1], 0.0, [R("m_reset", 1)])

    pcol_index = {}
    pst_r = R("scr_stage")
    pstage = scr[:, 2048:2816].rearrange("p (g c) -> p g c", g=6)
    MS("pool", scr[:, 2048:2816], 0.0, [pst_r])
    cursor = [0, 0]

    def stage(name, ap2d):
        n = ap2d.shape[0]
        if cursor[1] + n > 128:
            cursor[0] += 1
            cursor[1] = 0
        g, r0 = cursor
        kb.dma("pool", pstage[r0:r0 + n, g, :], ap2d, writes=[pst_r])
        for i in range(n):
            pcol_index[(name, i)] = g * 128 + r0 + i
        cursor[1] += n

    def v2(ap):
        return ap.rearrange("(k p) -> k p", p=128)

    stage("ng", norm_gain.rearrange("a b (k p) -> (a b k) p", p=128))
    stage("fng", v2(final_gain))
    if use_c:
        stage("cwc", conv_w_c.rearrange("a (k p) -> (a k) p", p=128))
        stage("cbc", v2(conv_b_c))
        stage("bga", v2(b_ga))
        stage("bgx", v2(b_gx))
        stage("lam", v2(lam))
    if use_ab:
        stage("cwa", conv_w_a.rearrange("a (k p) -> (a k) p", p=128))
        stage("cba", v2(conv_b_a))
        stage("gna", v2(gnorm))
        stage("mu", v2(mu_b))
        for nm, t in (("w0", w0_b), ("a0", a0_b), ("kk", k_k_b), ("ka", k_a_b), ("rk", r_k_b), ("lnw", ln_w_b), ("lnb", ln_b_b)):
            stage(nm, v2(t))
    ngrp = cursor[0] + 1
    assert ngrp <= 6
    for g in range(ngrp):
        pt, pr = (ps[6], R("ps", 6)) if g < 4 else (ps[7], R("ps", 7))
        TR(pt[:, (g % 4) * 128:(g % 4 + 1) * 128], pstage[:, g, :], [pst_r], [pr])
    CP("dve", params[:, 0:512], ps[6][:, :], [R("ps", 6)], [R("params")])
    if ngrp > 4:
        CP("dve", params[:, 512:768], ps[7][:, 0:256], [R("ps", 7)], [R("params")])
    RP = R("params")

    def pcol(name, i):
        c = pcol_index[(name, i)]
        return params[:, c:c + 1]

    dcols = sb("dcols", [128, 80])
    RD = R("dcols")
    if use_c:
        l0 = pcol_index[("lam", 0)]
        ACT(dcols[:, 0:16], params[:, l0:l0 + 16], AF.Exp, [RP], [RD], scale=-1.0)
        ACT(dcols[:, 0:16], dcols[:, 0:16], AF.Ln, [RD], [RD], bias=1.0)
        TS("dve", dcols[:, 0:16], dcols[:, 0:16], -8.0, None, MUL, None, [RD], [RD])
    if use_ab:
        w0i = pcol_index[("w0", 0)]
        kai = pcol_index[("ka", 0)]
        TS("dve", dcols[:, 16:32], params[:, w0i:w0i + 16], -1.0, None, MUL, None, [RP], [RD])
        TS("dve", dcols[:, 32:48], params[:, kai:kai + 16], -1.0, 1.0, MUL, ADD, [RP], [RD])
        dsv = d_skip.rearrange("(k t) -> t k", t=2)
        kb.dma("pool", dcols[0:64, 48:64], dsv[0, :].partition_broadcast(64), writes=[RD], slow=True)
        kb.dma("pool", dcols[64:128, 48:64], dsv[1, :].partition_broadcast(64), writes=[RD], slow=True)
        bcast = sb("bcast", [128, 64])
        kb.dma("pool", bcast[:, 0:32], dt_bias.partition_broadcast(128), writes=[R("bcast")])
        kb.dma("pool", bcast[:, 32:64], a_log.partition_broadcast(128), writes=[R("bcast")])
        ACT(bcast[:, 32:64], bcast[:, 32:64], AF.Exp, [R("bcast")], [R("bcast")])
        TS("dve", bcast[:, 32:64], bcast[:, 32:64], -1.0, None, MUL, None, [R("bcast")], [R("bcast")])
        wdt = sb("wdt", [128, KD, 32])
        kb.dma("sp", wdt[:], w_in_ab.rearrange("(k p) c -> p k c", p=128)[:, :, COL_DT:COL_DT + 32], writes=[R("wdt")], r32=True)

    def dcol(base, i):
        return dcols[:, base + i:base + i + 1]

    if use_c:
        lru_st = sb("lru_st", [128, 16])
        convc_tail = sb("convc_tail", [128, 16, 3])
        MS("dve", lru_st[:], 0.0, [R("lru_st")])
        MS("dve", convc_tail[:], 0.0, [R("convc_tail")])
    if use_ab:
        S_T = sb("S_T", [128, 2048])
        S0T = sb("S0T", [128, 16, 64])
        conva_tail = sb("conva_tail", [128, 24, 3])
        shift_tail = sb("shift_tail", [128, 50])
        MS("dve", S_T[:], 0.0, [R("S_T", i) for i in range(16)])
        MS("dve", S0T[:], 0.0, [R("S0T", i) for i in range(16)])
        MS("dve", conva_tail[:], 0.0, [R("conva_tail")])
        MS("dve", shift_tail[:], 0.0, [R("shift_tail")])

    xstage = scr[:, 0:D]
    RXS = R("xstage")

    def load_x(src, T):
        for tc in range(T // 128):
            kb.dma("pool", xstage, src[tc * 128:(tc + 1) * 128, :], writes=[RXS])
            for kk in range(4):
                pt, pr = ps[6 + (kk % 2)], R("ps", 6 + (kk % 2))
                for q in range(4):
                    k = kk * 4 + q
                    TR(pt[:, q * 128:(q + 1) * 128], xstage[:, k * 128:(k + 1) * 128], [RXS], [pr])
                CP("dve", x[:, kk * 4:kk * 4 + 4, tc * 128:(tc + 1) * 128], pt[:].rearrange("p (a b) -> p a b", a=4),
                   [pr], [R("x", kk * 4 + q) for q in range(4)])

    def rmsnorm(T, gname, gbase, out_r32=True):
        rstd, rr = T_(0)
        for k in range(KD):
            s, sr = Q_(k % 2)
            ACT(_r(s[:, :T]), x[:, k, :T], AF.Square, [R("x", k)], [sr])
            MM(ps[7][:, :T], _r(ones[:]), _r(s[:, :T]), k == 0, k == KD - 1, [sr, R("ones")], [R("ps", 7)])
        ACT(rstd[:, :T], ps[7][:, :T], AF.Sqrt, [R("ps", 7)], [rr], scale=1.0 / D, bias=EPS)
        op("dve", lambda e: e.reciprocal(out=rstd[:, :T], in_=rstd[:, :T]), reads=[rr], writes=[rr])
        for k in range(KD):
            if out_r32:
                o, wr = _r(hT[:, k, :T]), R("hT", k)
            else:
                o, wr = x[:, k, :T], R("x", k)
            STT(o, x[:, k, :T], pcol(gname, gbase + k), rstd[:, :T], MUL, MUL, [R("x", k), rr, RP], [wr])

    ffn_ctr = [0]

    def ffn(T, li, fi):
        rmsnorm(T, "ng", (li * 3 + (0 if fi == 0 else 2)) * KD)
        for (j0, j1) in ((0, 22), (22, NF)):
            for j in range(j0, j1):
                slot, sreg = kb.wnext(("ffn_in", li, fi, j))
                c = ffn_ctr[0]
                ffn_ctr[0] += 1
                pg, pu = ps[(c % 2) * 2], ps[(c % 2) * 2 + 1]
                rg, ru = R("ps", (c % 2) * 2), R("ps", (c % 2) * 2 + 1)
                wv = slot[:].rearrange("p (g k c) -> p g k c", g=2, k=KD)
                for g, (pp, rr) in enumerate(((pg, rg), (pu, ru))):
                    for k in range(KD):
                        MM(pp[:, :T], _r(wv[:, g, k, :]), _r(hT[:, k, :T]), k == 0, k == KD - 1, [sreg, R("hT", k)], [rr])
                st, str_ = T_(1 + c % 2)
                ACT(st[:, :T], pg[:, :T], AF.Silu, [rg], [str_])
                TTo("dve", _r(hid[:, j - j0, :T]), st[:, :T], pu[:, :T], MUL, [ru, str_], [R("hid", j - j0)])
            nf = j1 - j0
            for m in range(KD):
                slot, sreg = kb.wnext(("ffn_out", li, fi, j0, j1, m))
                c = ffn_ctr[0]
                ffn_ctr[0] += 1
                po, ro = ps[4 + (c % 2)], R("ps", 4 + (c % 2))
                wv = slot[:, :nf * 128].rearrange("p (j c) -> p j c", j=nf)
                for jj in range(nf):
                    MM(po[:, :T], _r(wv[:, jj, :]), _r(hid[:, jj, :T]), jj == 0, jj == nf - 1, [sreg, R("hid", jj)], [ro])
                STT(x[:, m, :T], po[:, :T], 0.5, x[:, m, :T], MUL, ADD, [ro, R("x", m)], [R("x", m)])

    def store_y(dst, T):
        rmsnorm(T, "fng", 0, out_r32=False)
        for tc in range(T // 128):
            for kk in range(4):
                pt, pr = ps[6 + (kk % 2)], R("ps", 6 + (kk % 2))
                for q in range(4):
                    k = kk * 4 + q
                    TR(pt[:, q * 128:(q + 1) * 128], x[:, k, tc * 128:(tc + 1) * 128], [R("x", k)], [pr])
                CP("act", xstage[:, kk * 512:(kk + 1) * 512], pt[:, :], [pr], [RXS])
            kb.dma("pool", dst[tc * 128:(tc + 1) * 128, :], xstage, reads=[RXS])

    def proj_chunk(slot, sreg, coff, T, c0, M=128):
        pp, pr = PS()
        return pp, pr

    def mixer_c(win):
        c0, T, nseg, L, kind = win["c0"], win["T"], win["nseg"], win["L"], win["kind"]
        smp = kind == "s"
        E = L + 3
        ymix = hidf[:, 0:KD * T].rearrange("p (k t) -> p k t", k=KD)
        if smp:
            o = 2048
            SCC = scr[:, o:o + 768].rearrange("p (c s k) -> p c s k", c=16, s=16)
            o += 768
            SLRU = scr[:, o:o + 256].rearrange("p (c s) -> p c s", c=16)
            o += 256
            CCO = scr[:, o:o + 768].rearrange("p (c s k) -> p c s k", c=16, s=16)
            o += 768
            LRUO = scr[:, o:o + 256].rearrange("p (c s) -> p c s", c=16)
            o += 256
            stg = scr[:, 0:D]
            kb.dma("pool", stg[0:48, :], st_convc, writes=[R("stg")])
            for ch in range(16):
                pp, pr = PS(6, 8)
                TR(pp[:, 0:48], stg[0:48, ch * 128:(ch + 1) * 128], [R("stg")], [pr])
                CP("dve", SCC[:, ch].rearrange("p s k -> p (s k)"), pp[:, 0:48], [pr], [R("SCC")])
            kb.dma("pool", stg[0:16, :], st_lru, writes=[R("stg")])
            for ch in range(16):
                pp, pr = PS(6, 8)
                TR(pp[:, 0:16], stg[0:16, ch * 128:(ch + 1) * 128], [R("stg")], [pr])
                CP("dve", SLRU[:, ch, :], pp[:, 0:16], [pr], [R("SLRU")])
        for h in range(8):
            slot, sreg = kb.wnext(("cols", "w_in_c", 0, ((D + h * 256, 256),)))
            wv = slot[:, :KD * 256].rearrange("p (k c) -> p k c", k=KD)
            xc = []
            for j in range(2):
                ch = 2 * h + j
                pp, pr = PS()
                for k in range(KD):
                    MM(pp[:, :T], _r(wv[:, k, j * 128:(j + 1) * 128]), _r(hT[:, k, c0:c0 + T]), k == 0, k == KD - 1, [sreg, R("hT", k)], [pr])
                ext, er = T_(3 + j)
                ev = ext[:, :nseg * E].rearrange("p (s e) -> p s e", s=nseg)
                if smp:
                    CP("dve", ev[:, :, 0:3], SCC[:, ch], [R("SCC")], [er])
                else:
                    CP("dve", ev[:, :, 0:3], convc_tail[:, ch:ch + 1, :], [R("convc_tail")], [er])
                CP("act", ev[:, :, 3:3 + L], v3(pp[:, :T], nseg), [pr], [er])
                if smp:
                    CP("dve", CCO[:, ch], ev[:, :, L:L + 3], [er], [R("CCO")])
                else:
                    CP("dve", convc_tail[:, ch:ch + 1, :], ev[:, :, L:L + 3], [er], [R("convc_tail")])
                acc, ar = T_(5 + j)
                a3 = v3(acc[:, :T], nseg)
                TS("dve", a3, ev[:, :, 0:L], pcol("cwc", ch), pcol("cbc", ch), MUL, ADD, [er, RP], [ar])
                for kk in (1, 2):
                    STT(a3, ev[:, :, kk:kk + L], pcol("cwc", kk * 16 + ch), a3, MUL, ADD, [er, ar, RP], [ar])
                xcj, xr = Q_(2 + j)
                STT(_r(v3(xcj[:, :T], nseg)), ev[:, :, 3:3 + L], pcol("cwc", 48 + ch), a3, MUL, ADD, [er, ar, RP], [xr])
                xc.append((xcj, xr))
            slot, sreg = kb.wnext(("gates", h))
            gv = slot[:, :1024].rearrange("p (g i c) -> p g i c", g=2, i=2)
            gts = {}
            for gi, bname in enumerate(("bga", "bgx")):
                for jc in range(2):
                    pp, pr = PS()
                    for ic in range(2):
                        MM(pp[:, :T], _r(gv[:, gi, ic, jc * 128:(jc + 1) * 128]), _r(xc[ic][0][:, :T]), ic == 0, ic == 1, [sreg, xc[ic][1]], [pr])
                    gt, gr = T_(9 + gi * 2 + jc)
                    ACT(gt[:, :T], pp[:, :T], AF.Sigmoid, [pr, RP], [gr], bias=pcol(bname, 2 * h + jc))
                    gts[(gi, jc)] = (gt, gr)
            slot, sreg = kb.wnext(("cols", "w_in_c", 0, ((h * 256, 256),)))
            wv = slot[:, :KD * 256].rearrange("p (k c) -> p k c", k=KD)
            for j in range(2):
                ch = 2 * h + j
                pp, pr = PS()
                for k in range(KD):
                    MM(pp[:, :T], _r(wv[:, k, j * 128:(j + 1) * 128]), _r(hT[:, k, c0:c0 + T]), k == 0, k == KD - 1, [sreg, R("hT", k)], [pr])
                rg, rgr = gts[(0, j)]
                ig, igr = gts[(1, j)]
                xcj, xr = xc[j]
                la, lar = T_(13)
                TS("dve", la[:, :T], rg[:, :T], dcol(0, ch), None, MUL, None, [rgr, RD], [lar])
                a, ar_ = T_(14)
                ACT(a[:, :T], la[:, :T], AF.Exp, [lar], [ar_])
                a2, a2r = T_(15)
                ACT(a2[:, :T], la[:, :T], AF.Exp, [lar], [a2r], scale=2.0)
                ACT(a2[:, :T], a2[:, :T], AF.Sqrt, [a2r], [a2r], scale=-1.0, bias=1.0)
                bt, btr = T_(16)
                TTo("dve", bt[:, :T], a2[:, :T], ig[:, :T], MUL, [a2r, igr], [btr])
                TTo("dve", bt[:, :T], bt[:, :T], xcj[:, :T], MUL, [btr, xr], [btr])
                a3, b3 = v3(a[:, :T], nseg), v3(bt[:, :T], nseg)
                t0, t0r = T_(17)
                if smp:
                    h0 = SLRU[:, ch, :].unsqueeze(2)
                    h0r = R("SLRU")
                else:
                    h0 = lru_st[:, ch:ch + 1].unsqueeze(2)
                    h0r = R("lru_st")
                tv = t0[:, :nseg].unsqueeze(2)
                TTo("dve", tv, a3[:, :, 0:1], h0, MUL, [ar_, h0r], [t0r])
                TTo("dve", b3[:, :, 0:1], b3[:, :, 0:1], tv, ADD, [btr, t0r], [btr])
                MS("dve", a3[:, :, 0:1], 0.0, [ar_])
                hh_, hr = T_(18)
                op("dve", lambda e, hh_=hh_, a=a, bt=bt: e.tensor_tensor_scan(out=hh_[:, :T], data0=a[:, :T], data1=bt[:, :T], initial=0.0,
                                                                               op0=MUL, op1=ADD), reads=[ar_, btr], writes=[hr])
                h3 = v3(hh_[:, :T], nseg)
                if smp:
                    CP("dve", LRUO[:, ch, :].unsqueeze(2), h3[:, :, L - 1:L], [hr], [R("LRUO")])
                else:
                    CP("dve", lru_st[:, ch:ch + 1].unsqueeze(2), h3[:, :, L - 1:L], [hr], [R("lru_st")])
                g2, g2r = T_(19)
                ACT(g2[:, :T], pp[:, :T], AF.Square, [pr], [g2r])
                TS("dve", g2[:, :T], g2[:, :T], 0.044715, 1.0, MUL, ADD, [g2r], [g2r])
                TTo("dve", g2[:, :T], g2[:, :T], pp[:, :T], MUL, [g2r, pr], [g2r])
                ACT(g2[:, :T], g2[:, :T], AF.Sigmoid, [g2r], [g2r], scale=1.5957691216)
                TTo("dve", g2[:, :T], g2[:, :T], pp[:, :T], MUL, [g2r, pr], [g2r])
                TTo("dve", _r(ymix[:, ch, :]), g2[:, :T], hh_[:, :T], MUL, [g2r, hr], [R("ymix", ch)])
        out_proj("w_out_c", 0, ymix, T, c0)
        if smp:
            for ch in range(16):
                pp, pr = PS(6, 8)
                TR(pp[0:16, 0:128], LRUO[:, ch, :], [R("LRUO")], [pr])
                CP("act", stg[0:16, ch * 128:(ch + 1) * 128], pp[0:16, 0:128], [pr], [R("stg")])
            kb.dma("pool", lru_s, stg[0:16, :], reads=[R("stg")])
            for ch in range(16):
                pp, pr = PS(6, 8)
                TR(pp[0:48, 0:128], CCO[:, ch].rearrange("p s k -> p (s k)"), [R("CCO")], [pr])
                CP("act", stg[0:48, ch * 128:(ch + 1) * 128], pp[0:48, 0:128], [pr], [R("stg")])
            kb.dma("pool", convc_s, stg[0:48, :], reads=[R("stg")])

    def out_proj(wname, row0, ymix, T, c0):
        for blk in range(8):
            slot, sreg = kb.wnext(("cols", wname, row0, ((blk * 256, 256),)))
            wv = slot[:, :KD * 256].rearrange("p (k c) -> p k c", k=KD)
            for j in range(2):
                m = 2 * blk + j
                pp, pr = PS()
                for k in range(KD):
                    MM(pp[:, :T], _r(wv[:, k, j * 128:(j + 1) * 128]), _r(ymix[:, k, :]), k == 0, k == KD - 1, [sreg, R("ymix", k)], [pr])
                TTo("dve", x[:, m, c0:c0 + T], x[:, m, c0:c0 + T], pp[:, :T], ADD, [pr, R("x", m)], [R("x", m)])

    def final_states_c(dst_lru, dst_conv):
        stg = scr[:, 0:D]
        pp, pr = PS(6, 8)
        TR(pp[0:16, 0:128], lru_st[:], [R("lru_st")], [pr])
        CP("act", stg[0:16, 0:128], pp[0:16, 0:128], [pr], [R("stg")])
        kb.dma("pool", dst_lru.rearrange("(k p) -> k p", p=128), stg[0:16, 0:128], reads=[R("stg")])
        for kk in range(3):
            pp, pr = PS(6, 8)
            TR(pp[0:16, 0:128], convc_tail[:, :, kk], [R("convc_tail")], [pr])
            CP("act", stg[0:16, 128 * (kk + 1):128 * (kk + 2)], pp[0:16, 0:128], [pr], [R("stg")])
            kb.dma("pool", dst_conv[kk].rearrange("(k p) -> k p", p=128), stg[0:16, 128 * (kk + 1):128 * (kk + 2)], reads=[R("stg")])

    if use_ab:
        snat = [sb("snat%d" % i, [128, 128]) for i in range(2)]
        sTt = [sb("sTt%d" % i, [128, 128]) for i in range(2)]
        cmk = [sb("cmk%d" % i, [128, 128]) for i in range(2)]
        bmk = [sb("bmk%d" % i, [128, 128]) for i in range(2)]
        snew = sb("snew", [128, 128])
        sout = [sb("sout%d" % i, [128, 128]) for i in range(2)]

    def PSI(i):
        return ps[i], R("ps", i)

    def proj128(slot, sreg, coff, tot, T, bank):
        pp, pr = PSI(bank)
        wv = slot[:, :KD * tot].rearrange("p (k c) -> p k c", k=KD)
        for k in range(KD):
            MM(pp[:, :T], _r(wv[:, k, coff:coff + 128]), _r(hT[:, k, 0:T]), k == 0, k == KD - 1, [sreg, R("hT", k)], [pr])
        return pp, pr

    def conv_a(pp, pr, cc, T, nseg, L, smp, SCA, CAO):
        E = L + 3
        ext, er = T_(8)
        ev = ext[:, :nseg * E].rearrange("p (s e) -> p s e", s=nseg)
        if smp:
            CP("dve", ev[:, :, 0:3], SCA[:, cc], [R("SCA")], [er])
        else:
            CP("dve", ev[:, :, 0:3], conva_tail[:, cc:cc + 1, :], [R("conva_tail")], [er])
        CP("act", ev[:, :, 3:3 + L], v3(pp[:, :T], nseg), [pr], [er])
        if smp:
            CP("dve", CAO[:, cc], ev[:, :, L:L + 3], [er], [R("CAO")])
        else:
            CP("dve", conva_tail[:, cc:cc + 1, :], ev[:, :, L:L + 3], [er], [R("conva_tail")])
        acc, ar = T_(9)
        a3 = v3(acc[:, :T], nseg)
        TS("dve", a3, ev[:, :, 0:L], pcol("cwa", cc), pcol("cba", cc), MUL, ADD, [er, RP], [ar])
        for kk in (1, 2, 3):
            STT(a3, ev[:, :, kk:kk + L], pcol("cwa", kk * 24 + cc), a3, MUL, ADD, [er, ar, RP], [ar])
        return acc, ar

    def mamba(win, ymix, BC, Btok, SCA, CAO):
        T, nseg, L, kind = win["T"], win["nseg"], win["L"], win["kind"]
        smp = kind == "s"
        mi = 1 if smp else 0
        nch = T // 128
        dt, dtr = T_(0)
        dta, dtar = T_(1)
        ecum, ecr = T_(2)
        tail, tlr = T_(3)
        decs = [T_(4), T_(5)]
        dtam = [T_(6), T_(7)]
        for c in range(nch):
            pp, pr = PSI(5)
            for k in range(KD):
                MM(pp[:, 0:32], _r(hT[:, k, c * 128:(c + 1) * 128]), _r(wdt[:, k, :]), k == 0, k == KD - 1, [R("hT", k), R("wdt")], [pr])
            sl = slice(c * 32, (c + 1) * 32)
            TTo("dve", dt[:, sl], pp[:, 0:32], bcast[:, 0:32], ADD, [pr, R("bcast")], [dtr])
            TS("dve", dt[:, sl], dt[:, sl], 30.0, None, MIN, None, [dtr], [dtr])
            ACT(dt[:, sl], dt[:, sl], AF.Exp, [dtr], [dtr])
            ACT(dt[:, sl], dt[:, sl], AF.Ln, [dtr], [dtr], bias=1.0)
            TTo("dve", dta[:, sl], dt[:, sl], bcast[:, 32:64], MUL, [dtr, R("bcast")], [dtar])
            pq, pqr = PSI(4)
            MM(pq[:, 0:32], m_iu[mi][:], dta[:, sl], True, True, [R("m_iu", mi), dtar], [pqr])
            MM(pq[:, 32:64], m_sl[mi][:], dta[:, sl], True, True, [R("m_sl", mi), dtar], [pqr])
            if not smp:
                MM(pq[:, 64:96], ones[:], dta[:, sl], True, True, [R("ones"), dtar], [pqr])
                ACT(decs[0][0][:, sl], pq[:, 64:96], AF.Exp, [pqr], [decs[0][1]])
            ACT(ecum[:, sl], pq[:, 0:32], AF.Exp, [pqr], [ecr])
            ACT(tail[:, sl], pq[:, 32:64], AF.Exp, [pqr], [tlr])
            if smp:
                pd, pdr = PSI(3)
                for t in range(2):
                    dm, dmr = dtam[t]
                    TTo("dve", dm[:, 0:256].rearrange("p (s h) -> p s h", s=8), dta[:, 0:32].unsqueeze(1).to_broadcast([128, 8, 32]),
                        segmask[:, 8 * t:8 * t + 8].unsqueeze(2).to_broadcast([128, 8, 32]), MUL, [dtar, R("segmask")], [dmr])
                    MM(pd[:, t * 256:(t + 1) * 256], ones[:], dm[:, 0:256], True, True, [R("ones"), dmr], [pdr])
                    ACT(decs[t][0][:, 0:256], pd[:, t * 256:(t + 1) * 256], AF.Exp, [pdr], [decs[t][1]])

        def dec_ap(c, s, h0):
            if smp:
                return decs[s // 8][0][:, (s % 8) * 32 + h0:(s % 8) * 32 + h0 + 2], decs[s // 8][1]
            return decs[0][0][:, c * 32 + h0:c * 32 + h0 + 2], decs[0][1]

        for blk in range(4):
            slot, sreg = kb.wnext(("cols", "w_in_ab", 0, ((COL_B + blk * 256, 256),)))
            for j in range(2):
                i = blk * 2 + j
                pp, pr = proj128(slot, sreg, j * 128, 256, T, 5)
                acc, ar = conv_a(pp, pr, 16 + i, T, nseg, L, smp, SCA, CAO)
                ACT(_r(BC[:, i, :]), acc[:, :T], AF.Silu, [ar], [R("BC", i)])
        cbt = {}
        for c in range(nch):
            cs = slice(c * 128, (c + 1) * 128)
            for g in range(4):
                pt, ptr = PSI(6 + g % 2)
                TR(pt[:, 0:128], BC[:, g, cs], [R("BC", g)], [ptr])
                CP("act", _r(Btok[:, c, g, :]), pt[:, 0:128], [ptr], [R("Btok")])
                pc, pcr = PSI(4)
                MM(pc[:, 0:128], _r(BC[:, g, cs]), _r(BC[:, 4 + g, cs]), True, True, [R("BC", g), R("BC", 4 + g)], [pcr])
                ti = 12 + (c * 4 + g) // 2
                ct, ctr = T_(ti)
                o = ((c * 4 + g) % 2) * 128
                CP("dve", ct[:, o:o + 128], pc[:, 0:128], [pcr], [ctr])
                cbt[(c, g)] = (ct[:, o:o + 128], ctr)
        if smp:
            for i in range(2):
                MS("dve", cmk[i][:], 0.0, [R("cmk", i)])
        ytok, ytr = T_(19)
        for hh in range(16):
            g = hh // 4
            slot, sreg = kb.wnext(("cols", "w_in_ab", 0, ((COL_XS + hh * 128, 128), (COL_Z + hh * 128, 128))))
            pp, pr = proj128(slot, sreg, 0, 256, T, 5)
            acc, ar = conv_a(pp, pr, hh, T, nseg, L, smp, SCA, CAO)
            xsh, xsr = T_(10)
            ACT(xsh[:, :T], acc[:, :T], AF.Silu, [ar], [xsr])
            pp, pr = proj128(slot, sreg, 128, 256, T, 5)
            zs, zsr = T_(11)
            ACT(zs[:, :T], pp[:, :T], AF.Silu, [pr], [zsr])
            for c in range(nch):
                cs = slice(c * 128, (c + 1) * 128)
                pt, ptr = PSI(6)
                TR(pt[:, 0:128], xsh[:, cs], [xsr], [ptr])
                xdt, xdr = Q_(2)
                xdtt, xtr = Q_(3)
                h0 = c * 32 + 2 * hh
                TTo("dve", _r(xdt[:, 0:128].rearrange("p (h q) -> p h q", h=2)), pt[:, 0:128].rearrange("p (h q) -> p h q", h=2),
                    dt[:, h0:h0 + 2].unsqueeze(2).to_broadcast([128, 2, 64]), MUL, [ptr, dtr], [xdr])
                TTo("dve", _r(xdtt[:, 0:128].rearrange("p (h q) -> p h q", h=2)), xdt[:, 0:128].rearrange("p (h q) -> p h q", h=2),
                    tail[:, h0:h0 + 2].unsqueeze(2).to_broadcast([128, 2, 64]), MUL, [xdr, tlr], [xtr])
                yd, ydr = PSI(2)
                for h2 in range(2):
                    h = h0 + h2
                    lt, ltr = T_(16)
                    TS("dve", lt[:, 0:128], m_sl[mi][:], dta[:, h:h + 1], None, MUL, None, [R("m_sl", mi), dtar], [ltr])
                    pe_, per = PSI(h2)
                    MM(pe_[:, 0:128], lt[:, 0:128], m_iu[mi][:], True, False, [ltr, R("m_iu", mi)], [per])
                    MM(pe_[:, 0:128], ident[:], m_neg[mi][:], False, True, [R("ident"), R("m_neg", mi)], [per])
                    lm, lmr = T_(17)
                    ACT(lm[:, 0:128], pe_[:, 0:128], AF.Exp, [per], [lmr])
                    G, Gr = Q_(4)
                    ct, ctr = cbt[(c, g)]
                    TTo("dve", _r(G[:, 0:128]), lm[:, 0:128], ct, MUL, [lmr, ctr], [Gr])
                    MM(yd[:, h2 * 64:(h2 + 1) * 64], _r(G[:, 0:128]), _r(xdt[:, h2 * 64:(h2 + 1) * 64]), True, True, [Gr, xdr], [ydr])
                yo, yor = PSI(3)
                hs = slice(hh * 128, (hh + 1) * 128)
                if not smp:
                    MM(yo[:, 0:128], _r(BC[:, 4 + g, cs]), _r(S_T[:, hs]), True, True, [R("BC", 4 + g), R("S_T", hh)], [yor])
                else:
                    for s in range(NSEG):
                        nat, natr = snat[s % 2], R("snat", s % 2)
                        kb.dma("pool", nat[:], st_ssm[s, 2 * hh:2 * hh + 2].rearrange("h p n -> (h p) n"), writes=[natr])
                        pt2, pt2r = PSI(7)
                        TR(pt2[:, 0:128], nat[:], [natr], [pt2r])
                        sT, sTr = sTt[s % 2], R("sTt", s % 2)
                        CP("act", _r(sT[:]), pt2[:, 0:128], [pt2r], [sTr])
                        cm, cmr = cmk[s % 2], R("cmk", s % 2)
                        ss = slice(s * 8, (s + 1) * 8)
                        CP("dve", _r(cm[:, ss]), BC[:, 4 + g, ss], [R("BC", 4 + g)], [cmr])
                        MM(yo[:, 0:128], _r(cm[:]), _r(sT[:]), s == 0, s == NSEG - 1, [cmr, sTr], [yor])
                        MS("dve", cm[:, ss], 0.0, [cmr])
                        bm, bmr = bmk[s % 2], R("bmk", s % 2)
                        TS("dve", _r(bm[:]), Btok[:, 0, g, :], segmask[:, s:s + 1], None, MUL, None, [R("Btok"), R("segmask")], [bmr])
                        pn, pnr = PSI(4)
                        MM(pn[:, 0:128], _r(bm[:]), _r(xdtt[:, 0:128]), True, True, [bmr, xtr], [pnr])
                        dap, dr_ = dec_ap(0, s, 2 * hh)
                        TTo("dve", snew[:].rearrange("p (h q) -> p h q", h=2), sT[:].rearrange("p (h q) -> p h q", h=2),
                            dap.unsqueeze(2).to_broadcast([128, 2, 64]), MUL, [sTr, dr_], [R("snew")])
                        TTo("dve", snew[:], snew[:], pn[:, 0:128], ADD, [R("snew"), pnr], [R("snew")])
                        pt3, pt3r = PSI(6)
                        TR(pt3[:, 0:128], snew[:], [R("snew")], [pt3r])
                        so, sor = sout[s % 2], R("sout", s % 2)
                        CP("act", so[:], pt3[:, 0:128], [pt3r], [sor])
                        kb.dma("pool", ssm_s[s, 2 * hh:2 * hh + 2].rearrange("h p n -> (h p) n"), so[:], reads=[sor])
                for h2 in range(2):
                    h = h0 + h2
                    t18, t18r = T_(18)
                    ACT(t18[:, 0:64], yo[:, h2 * 64:(h2 + 1) * 64], AF.Copy, [yor, ecr], [t18r], scale=ecum[:, h:h + 1])
                    TTo("dve", ytok[:, c * 128 + h2 * 64:c * 128 + (h2 + 1) * 64], t18[:, 0:64], yd[:, h2 * 64:(h2 + 1) * 64], ADD,
                        [t18r, ydr], [ytr])
                if not smp:
                    pn, pnr = PSI(4)
                    MM(pn[:, 0:128], _r(Btok[:, c, g, :]), _r(xdtt[:, 0:128]), True, True, [R("Btok"), xtr], [pnr])
                    dap, dr_ = dec_ap(c, 0, 2 * hh)
                    TTo("dve", _r(S_T[:, hs].rearrange("p (h q) -> p h q", h=2)), S_T[:, hs].rearrange("p (h q) -> p h q", h=2),
                        dap.unsqueeze(2).to_broadcast([128, 2, 64]), MUL, [R("S_T", hh), dr_], [R("S_T", hh)])
                    TTo("dve", _r(S_T[:, hs]), S_T[:, hs], pn[:, 0:128], ADD, [R("S_T", hh), pnr], [R("S_T", hh)])
            for c in range(nch):
                cs = slice(c * 128, (c + 1) * 128)
                pt, ptr = PSI(7)
                TR(pt[:, 0:128], ytok[:, cs], [ytr], [ptr])
                t18, t18r = T_(18)
                STT(t18[:, 0:128], xsh[:, cs], dcol(48, hh), pt[:, 0:128], MUL, ADD, [xsr, RD, ptr], [t18r])
                TTo("dve", _r(ymix[:, hh, cs]), t18[:, 0:128], zs[:, cs], MUL, [t18r, zsr], [R("ymix", hh)])
            if hh % 4 == 3:
                for q in range(4):
                    sqt, sqr = Q_(q % 2)
                    ACT(_r(sqt[:, :T]), ymix[:, 4 * g + q, :], AF.Square, [R("ymix", 4 * g + q)], [sqr])
                    MM(ps[5][:, :T], _r(ones[:]), _r(sqt[:, :T]), q == 0, q == 3, [sqr, R("ones")], [R("ps", 5)])
                rs_, rsr = T_(18)
                ACT(rs_[:, :T], ps[5][:, :T], AF.Sqrt, [R("ps", 5)], [rsr], scale=1.0 / 512, bias=EPS)
                op("dve", lambda e, rs_=rs_: e.reciprocal(out=rs_[:, :T], in_=rs_[:, :T]), reads=[rsr], writes=[rsr])
                for q in range(4):
                    ch = 4 * g + q
                    STT(_r(ymix[:, ch, :]), ymix[:, ch, :], pcol("gna", ch), rs_[:, :T], MUL, MUL, [R("ymix", ch), rsr, RP], [R("ymix", ch)])

    def mixer_ab(win):
        T, nseg, L, kind = win["T"], win["nseg"], win["L"], win["kind"]
        smp = kind == "s"
        nch = T // 128
        ymix = hidf[:, 0:KD * T].rearrange("p (k t) -> p k t", k=KD)
        BC = hidf[:, 4096:4096 + 8 * T].rearrange("p (k t) -> p k t", k=8)
        Btok = hidf[:, 6144:6144 + nch * 512].rearrange("p (c g n) -> p c g n", c=nch, g=4)
        SCA = CAO = SSH = SHO = None
        stg = scr[:, 0:D]
        if smp:
            SCA = scr[:, 2048:3200].rearrange("p (c s k) -> p c s k", c=24, s=16)
            CAO = scr[:, 3200:4352].rearrange("p (c s k) -> p c s k", c=24, s=16)
            SSH = scr[:, 4352:5152].rearrange("p (c s) -> p c s", c=50)
            SHO = scr[:, 5152:5952].rearrange("p (c s) -> p c s", c=50)
            for (c0_, n_) in ((0, 16), (16, 8)):
                kb.dma("pool", stg[0:48, 0:n_ * 128], st_conva[:, c0_ * 128:(c0_ + n_) * 128], writes=[R("stg")])
                for ch in range(n_):
                    pp, pr = PSI(6 + ch % 2)
                    TR(pp[:, 0:48], stg[0:48, ch * 128:(ch + 1) * 128], [R("stg")], [pr])
                    CP("dve", SCA[:, c0_ + ch].rearrange("p s k -> p (s k)"), pp[:, 0:48], [pr], [R("SCA")])
        if "a" in cfg.get("ab_parts", "ab"):
            mamba(win, ymix, BC, Btok, SCA, CAO)
            out_proj("w_out_ab", 0, ymix, T, 0)
            kb.barrier()
        if "b" in cfg.get("ab_parts", "ab"):
            rwkv(win, ymix, SSH, SHO)
            out_proj("w_out_ab", D, ymix, T, 0)
            kb.barrier()
        if smp:
            if "a" in cfg.get("ab_parts", "ab"):
                for (c0_, n_) in ((0, 16), (16, 8)):
                    for ch in range(n_):
                        pp, pr = PSI(6 + ch % 2)
                        TR(pp[0:48, 0:128], CAO[:, c0_ + ch].rearrange("p s k -> p (s k)"), [R("CAO")], [pr])
                        CP("act", stg[0:48, ch * 128:(ch + 1) * 128], pp[0:48, 0:128], [pr], [R("stg")])
                    kb.dma("pool", conva_s[:, c0_ * 128:(c0_ + n_) * 128], stg[0:48, 0:n_ * 128], reads=[R("stg")])

    def final_states_ab():
        stg = scr[:, 0:D]
        if "a" in cfg.get("ab_parts", "ab"):
            for hh in range(16):
                pt, ptr = PSI(6 + hh % 2)
                TR(pt[:, 0:128], S_T[:, hh * 128:(hh + 1) * 128], [R("S_T", hh)], [ptr])
                so, sor = sout[hh % 2], R("sout", hh % 2)
                CP("act", so[:], pt[:, 0:128], [ptr], [sor])
                kb.dma("pool", ssm_p[2 * hh:2 * hh + 2].rearrange("h p n -> (h p) n"), so[:], reads=[sor])
            for kk in range(3):
                pp, pr = PSI(6 + kk % 2)
                TR(pp[0:24, 0:128], conva_tail[:, :, kk], [R("conva_tail")], [pr])
                CP("act", stg[0:24, 128 * kk:128 * (kk + 1)], pp[0:24, 0:128], [pr], [R("stg")])
                kb.dma("pool", conva_p[kk].rearrange("(k p) -> k p", p=128), stg[0:24, 128 * kk:128 * (kk + 1)], reads=[R("stg")])
        if "b" in cfg.get("ab_parts", "ab"):
            final_states_b()

    def shift_chunk(pp, pr, ci, T, nseg, L, smp, SSH, SHO, out, outr, r32out=False, func=None, rows=None):
        E = L + 1
        ext, er = T_(0)
        ev = ext[:, :nseg * E].rearrange("p (s e) -> p s e", s=nseg)
        if smp:
            CP("dve", ev[:, :, 0:1], SSH[:, ci, :].unsqueeze(2), [R("SSH")], [er])
        else:
            CP("dve", ev[:, :, 0:1], shift_tail[:, ci:ci + 1].unsqueeze(2), [R("shift_tail")], [er])
        CP("act", ev[:, :, 1:1 + L], v3(pp[:, :T], nseg), [pr], [er])
        if smp:
            CP("dve", SHO[:, ci, :].unsqueeze(2), ev[:, :, L:L + 1], [er], [R("SHO")])
        else:
            CP("dve", shift_tail[:, ci:ci + 1].unsqueeze(2), ev[:, :, L:L + 1], [er], [R("shift_tail")])
        d, dr = T_(12)
        d3 = v3(d[:, :T], nseg)
        TTo("dve", d3, ev[:, :, 0:L], ev[:, :, 1:1 + L], SUB, [er], [dr])
        STT(v3(out[:, :T], nseg), d3, pcol("mu", ci), ev[:, :, 1:1 + L], MUL, ADD, [dr, er, RP], [outr])

    def rwkv(win, ymix, SSH, SHO):
        T, nseg, L, kind = win["T"], win["nseg"], win["L"], win["kind"]
        smp = kind == "s"
        mi = 1 if smp else 0
        nch = T // 128
        nit = 2 if smp else 6
        stg = scr[:, 0:D]
        if smp:
            for b in range(4):
                nb = min(16, 50 - 16 * b)
                kb.dma("pool", stg[0:16, 0:nb * 128], st_shift[:, b * 2048:b * 2048 + nb * 128], writes=[R("stg")])
                for ch in range(nb):
                    pp, pr = PSI(6 + ch % 2)
                    TR(pp[:, 0:16], stg[0:16, ch * 128:(ch + 1) * 128], [R("stg")], [pr])
                    CP("dve", SSH[:, 16 * b + ch, :], pp[:, 0:16], [pr], [R("SSH")])
        TX, TXr = Q_(2)
        SG, SGr = Q_(3)
        slot, sreg = kb.wnext(("cols", "w_in_ab", 0, ((COL_LW, 256),)))
        pp, pr = proj128(slot, sreg, 0, 256, T, 5)
        t1, t1r = T_(1)
        shift_chunk(pp, pr, 48, T, nseg, L, smp, SSH, SHO, t1, t1r)
        ACT(_r(TX[0:64, :T]), t1[0:64, :T], AF.Tanh, [t1r], [TXr])
        CP("dve", _r(TX[64:128, :T]), t1[64:128, :T], [t1r], [TXr])
        pp, pr = proj128(slot, sreg, 128, 256, T, 5)
        shift_chunk(pp, pr, 49, T, nseg, L, smp, SSH, SHO, t1, t1r)
        ACT(_r(SG[:, :T]), t1[:, :T], AF.Sigmoid, [t1r], [SGr])
        otok, otr = T_(15)
        for hh in range(16):
            slotA, srA = kb.wnext(("cols", "w_in_ab", 0, ((COL_R + hh * 128, 128), (COL_K + hh * 128, 128))))
            rs, rsr = T_(1)
            ks, ksr = T_(2)
            vs, vsr = T_(3)
            pp, pr = proj128(slotA, srA, 0, 256, T, 5)
            shift_chunk(pp, pr, hh, T, nseg, L, smp, SSH, SHO, rs, rsr)
            pp, pr = proj128(slotA, srA, 128, 256, T, 5)
            shift_chunk(pp, pr, 16 + hh, T, nseg, L, smp, SSH, SHO, ks, ksr)
            slotB, srB = kb.wnext(("lora", hh))
            pp, pr = proj128(slotB, srB, 0, 128, T, 5)
            shift_chunk(pp, pr, 32 + hh, T, nseg, L, smp, SSH, SHO, vs, vsr)
            lw = 2048
            pd, pdr = PSI(4)
            MM(pd[:, :T], _r(slotB[0:64, lw:lw + 128]), _r(TX[0:64, :T]), True, True, [srB, TXr], [pdr])
            ew, ewr = T_(4)
            ACT(ew[:, :T], pd[:, :T], AF.Exp, [pdr, RD], [ewr], scale=-1.0, bias=dcol(16, hh))
            ACT(ew[:, :T], ew[:, :T], AF.Ln, [ewr], [ewr], bias=1.0)
            ACT(ew[:, :T], ew[:, :T], AF.Exp, [ewr], [ewr], scale=-1.0, bias=-0.5)
            cg, cgr = T_(5)
            op("dve", lambda e, cg=cg, ew=ew: e.tensor_tensor_scan(out=cg[:, :T], data0=m_reset[mi][:, :T], data1=ew[:, :T], initial=0.0,
                                                                     op0=MUL, op1=ADD), reads=[ewr, R("m_reset", mi)], writes=[cgr])
            gam, gmr = T_(6)
            ig, igr = T_(7)
            gp, gpr = T_(8)
            ACT(gam[:, :T], cg[:, :T], AF.Exp, [cgr], [gmr], scale=-1.0)
            ACT(ig[:, :T], cg[:, :T], AF.Exp, [cgr], [igr])
            TTo("dve", gp[:, :T], ew[:, :T], cg[:, :T], SUB, [ewr, cgr], [gpr])
            ACT(gp[:, :T], gp[:, :T], AF.Exp, [gpr], [gpr])
            pa, par = PSI(4)
            MM(pa[:, :T], _r(slotB[64:128, lw:lw + 128]), _r(TX[64:128, :T]), True, True, [srB, TXr], [par])
            al, alr = T_(9)
            ACT(al[:, :T], pa[:, :T], AF.Sigmoid, [par, RP], [alr], bias=pcol("a0", hh))
            kkn, kkr = T_(10)
            TS("dve", kkn[:, :T], ks[:, :T], pcol("kk", hh), None, MUL, None, [ksr, RP], [kkr])
            sq, sqr = Q_(0)
            ACT(_r(sq[:, :T]), kkn[:, :T], AF.Square, [kkr], [sqr])
            pn_, pnr = PSI(4)
            MM(pn_[:, :T], _r(bones[:]), _r(sq[:, :T]), True, True, [R("bones"), sqr], [pnr])
            t12, t12r = T_(12)
            ACT(t12[:, :T], pn_[:, :T], AF.Sqrt, [pnr], [t12r])
            TS("dve", t12[:, :T], t12[:, :T], 1e-12, None, MAX, None, [t12r], [t12r])
            op("dve", lambda e, t12=t12: e.reciprocal(out=t12[:, :T], in_=t12[:, :T]), reads=[t12r], writes=[t12r])
            TTo("dve", kkn[:, :T], kkn[:, :T], t12[:, :T], MUL, [kkr, t12r], [kkr])
            kp, kpr = T_(11)
            TS("dve", kp[:, :T], al[:, :T], pcol("ka", hh), dcol(32, hh), MUL, ADD, [alr, RP, RD], [kpr])
            TTo("dve", kp[:, :T], kp[:, :T], ks[:, :T], MUL, [kpr, ksr], [kpr])
            AR = [Q_(4 + c) for c in range(nch)]
            for c in range(nch):
                cs = slice(c * 128, (c + 1) * 128)
                TTo("dve", _r(AR[c][0][:, 0:128]), kkn[:, cs], gp[:, cs], MUL, [kkr, gpr], [AR[c][1]])
                TTo("dve", _r(AR[c][0][:, 128:256]), rs[:, cs], gam[:, cs], MUL, [rsr, gmr], [AR[c][1]])
            BT, BTr = Q_(6)
            KT, KTr = Q_(7)
            TTo("dve", t12[:, :T], kkn[:, :T], al[:, :T], MUL, [kkr, alr], [t12r])
            STT(_r(BT[:, :T]), t12[:, :T], -1.0, ig[:, :T], MUL, MUL, [t12r, igr], [BTr])
            TTo("dve", _r(KT[:, :T]), kp[:, :T], ig[:, :T], MUL, [kpr, igr], [KTr])
            TTo("dve", t12[:, :T], rs[:, :T], kp[:, :T], MUL, [rsr, kpr], [t12r])
            sq1, sq1r = Q_(1)
            TS("dve", _r(sq1[:, :T]), t12[:, :T], pcol("rk", hh), None, MUL, None, [t12r, RP], [sq1r])
            pb, pbr = PSI(4)
            MM(pb[:, :T], _r(bones[:]), _r(sq1[:, :T]), True, True, [R("bones"), sq1r], [pbr])
            bon, bonr = T_(13)
            TTo("dve", bon[:, :T], vs[:, :T], pb[:, :T], MUL, [vsr, pbr], [bonr])
            pg, pgr = PSI(4)
            MM(pg[:, :T], _r(slotB[:, lw + 128:lw + 256]), _r(SG[:, :T]), True, True, [srB, SGr], [pgr])
            gg, ggr = T_(14)
            CP("act", gg[:, :T], pg[:, :T], [pgr], [ggr])
            Vtok, Vtr = Q_(8)
            Btk, Btkr = Q_(9)
            Ktk, Ktkr = Q_(10)
            for c in range(nch):
                cs = slice(c * 128, (c + 1) * 128)
                for (src, srcr, dst, dstr, bank) in ((vs, vsr, Vtok, Vtr, 6), (BT, BTr, Btk, Btkr, 7), (KT, KTr, Ktk, Ktkr, 6)):
                    pt, ptr = PSI(bank)
                    TR(pt[:, 0:128], src[:, cs], [srcr], [ptr])
                    CP("act", _r(dst[:, cs]), pt[:, 0:128], [ptr], [dstr])
            if smp:
                for s in range(NSEG):
                    nat, natr = snat[s % 2], R("snat", s % 2)
                    kb.dma("pool", nat[0:64, :].rearrange("p (h k) -> p h k", h=2), st_wkv[s, 2 * hh:2 * hh + 2].rearrange("h v k -> v h k"),
                           writes=[natr])
                    pt, ptr = PSI(6 + s % 2)
                    TR(pt[:, 0:64], nat[0:64, :], [natr], [ptr])
                    CP("act", _r(S0T[:, s, :]), pt[:, 0:64], [ptr], [R("S0T", s)])
                segs = [(s, slice(s * 8, (s + 1) * 8), s * 8 + 7) for s in range(NSEG)]
            for c in range(nch):
                cs = slice(c * 128, (c + 1) * 128)
                if not smp:
                    segs = [(hh, slice(0, 128), 127)]
                ARc, ARr = AR[c]
                NA = [Q_(11), Q_(12)]
                KA = [Q_(13), Q_(14)]
                Pm = [Q_(15), Q_(16)]
                Xm = [Q_(17), Q_(18)]
                Zs, Zsr = Q_(19)
                Up, Upr = Q_(20)
                for h2 in range(2):
                    rows = slice(h2 * 64, h2 * 64 + 64)
                    bp, bpr = PSI(h2 * 2)
                    bx, bxr = PSI(h2 * 2 + 1)
                    MM(bp[:, 0:128], _r(ARc[rows, 0:128]), _r(BT[rows, cs]), True, True, [ARr, BTr], [bpr])
                    TTo("dve", _r(Pm[h2][0][:, 0:128]), bp[:, 0:128], m_sl[mi][:], MUL, [bpr, R("m_sl", mi)], [Pm[h2][1]])
                    MM(bx[:, 0:256], _r(BT[rows, cs]), _r(ARc[rows, 0:256]), True, True, [ARr, BTr], [bxr])
                    TTo("dve", _r(NA[h2][0][:, 0:256]), bx[:, 0:256], m_pair[mi][:], MUL, [bxr, R("m_pair", mi)], [NA[h2][1]])
                    MM(bx[:, 0:256], _r(KT[rows, cs]), _r(ARc[rows, 0:256]), True, True, [ARr, KTr], [bxr])
                    TTo("dve", _r(KA[h2][0][:, 0:256]), bx[:, 0:256], m_pair[mi][:], MUL, [bxr, R("m_pair", mi)], [KA[h2][1]])
                    CP("act", _r(Xm[h2][0][:, 0:128]), NA[h2][0][:, 0:128], [NA[h2][1]], [R("XP", h2)])
                    TTo("dve", _r(Xm[h2][0][:, 128:256]), NA[h2][0][:, 0:128], ident[:], ADD, [NA[h2][1], R("ident")], [R("XW", h2)])
                for it in range(nit):
                    for h2 in range(2):
                        bp, bpr = PSI(h2 * 2)
                        bx, bxr = PSI(h2 * 2 + 1)
                        P_, Pr_ = Pm[h2]
                        X_ = Xm[h2][0]
                        last = it == nit - 1
                        MM(bp[:, 0:128], _r(X_[:, 0:128]), _r(P_[:, 0:128]), True, True, [R("XP", h2), Pr_], [bpr])
                        if not last:
                            MM(bx[:, 0:128], _r(P_[:, 0:128]), _r(X_[:, 0:128]), True, True, [R("XP", h2), Pr_], [bxr])
                        CP("act", _r(P_[:, 0:128]), bp[:, 0:128], [bpr], [Pr_])
                        if not last:
                            CP("act", _r(X_[:, 0:128]), bx[:, 0:128], [bxr], [R("XP", h2)])
                        MM(bx[:, 128:256], _r(P_[:, 0:128]), _r(X_[:, 128:256]), True, True, [Pr_, R("XW", h2)], [bxr])
                        TTo("dve", _r(X_[:, 128:256]), X_[:, 128:256], bx[:, 128:256], ADD, [R("XW", h2), bxr], [R("XW", h2)])
                for h2 in range(2):
                    rows = slice(h2 * 64, h2 * 64 + 64)
                    vcols = slice(c * 128 + h2 * 64, c * 128 + h2 * 64 + 64)
                    b4, b4r = PSI(4)
                    MM(b4[0:64, 0:128], _r(Vtok[:, vcols]), _r(KA[h2][0][:, 0:128]), True, False, [Vtr, KA[h2][1]], [b4r])
                    for si, (idx, sc, endc) in enumerate(segs):
                        MM(b4[0:64, sc], _r(S0T[rows, idx, :]), _r(ARc[rows, sc]), False, si == len(segs) - 1, [R("S0T", idx), ARr], [b4r])
                    zt, ztr = T_(18)
                    CP("act", zt[0:64, 0:128], b4[0:64, 0:128], [b4r], [ztr])
                    pt, ptr = PSI(6)
                    TR(pt[:, 0:64], zt[0:64, 0:128], [ztr], [ptr])
                    CP("act", _r(Zs[:, h2 * 64:(h2 + 1) * 64]), pt[:, 0:64], [ptr], [Zsr])
                    b5, b5r = PSI(5)
                    MM(b5[:, 0:64], _r(Xm[h2][0][:, 128:256]), _r(Zs[:, h2 * 64:(h2 + 1) * 64]), True, True, [R("XW", h2), Zsr], [b5r])
                    CP("act", _r(Up[:, h2 * 64:(h2 + 1) * 64]), b5[:, 0:64], [b5r], [Upr])
                    b4, b4r = PSI(4)
                    MM(b4[0:64, 0:128], _r(Up[:, h2 * 64:(h2 + 1) * 64]), _r(NA[h2][0][:, 128:256]), True, False, [Upr, NA[h2][1]], [b4r])
                    MM(b4[0:64, 0:128], _r(Vtok[:, vcols]), _r(KA[h2][0][:, 128:256]), False, False, [Vtr, KA[h2][1]], [b4r])
                    for si, (idx, sc, endc) in enumerate(segs):
                        sc2 = slice(128 + sc.start, 128 + sc.stop)
                        MM(b4[0:64, sc], _r(S0T[rows, idx, :]), _r(ARc[rows, sc2]), False, si == len(segs) - 1, [R("S0T", idx), ARr], [b4r])
                    CP("act", zt[0:64, 0:128], b4[0:64, 0:128], [b4r], [ztr])
                    pt, ptr = PSI(7)
                    TR(pt[:, 0:64], zt[0:64, 0:128], [ztr], [ptr])
                    CP("act", otok[:, vcols], pt[:, 0:64], [ptr], [otr])
                for (idx, sc, endc) in segs:
                    if smp:
                        bm, bmr = bmk[0], R("bmk", 0)
                        km, kmr = bmk[1], R("bmk", 1)
                        TS("dve", _r(bm[:]), Btk[:, cs], segmask[:, idx:idx + 1], None, MUL, None, [Btkr, R("segmask")], [bmr])
                        TS("dve", _r(km[:]), Ktk[:, cs], segmask[:, idx:idx + 1], None, MUL, None, [Ktkr, R("segmask")], [kmr])
                        bl, blr, kl, klr = bm[:], bmr, km[:], kmr
                    else:
                        bl, blr, kl, klr = Btk[:, cs], Btkr, Ktk[:, cs], Ktkr
                    b5, b5r = PSI(5)
                    MM(b5[:, 0:128], _r(bl), _r(Up[:, 0:128]), True, False, [blr, Upr], [b5r])
                    MM(b5[:, 0:128], _r(kl), _r(Vtok[:, cs]), False, True, [klr, Vtr], [b5r])
                    ec = c * 128 + endc
                    for h2 in range(2):
                        rows = slice(h2 * 64, h2 * 64 + 64)
                        t16, t16r = T_(16)
                        TTo("dve", t16[rows, 0:64], S0T[rows, idx, :], b5[rows, h2 * 64:(h2 + 1) * 64], ADD, [R("S0T", idx), b5r], [t16r])
                        TS("dve", _r(S0T[rows, idx, :]), t16[rows, 0:64], gam[rows, ec:ec + 1], None, MUL, None, [t16r, gmr], [R("S0T", idx)])
                o3 = otok[:, cs].rearrange("p (h q) -> p h q", h=2)
                st_, str_ = T_(17)
                op("dve", lambda e, o3=o3, st_=st_: e.tensor_reduce(out=st_[:, 0:2], in_=o3, axis=AX.X, op=ADD), reads=[otr], writes=[str_])
                TS("dve", st_[:, 0:2], st_[:, 0:2], 1.0 / 64, None, MUL, None, [str_], [str_])
                cen, cenr = T_(16)
                c3 = cen[:, 0:128].rearrange("p (h q) -> p h q", h=2)
                TTo("dve", c3, o3, st_[:, 0:2].unsqueeze(2).to_broadcast([128, 2, 64]), SUB, [otr, str_], [cenr])
                t19, t19r = T_(19)
                TTo("dve", t19[:, 0:128], cen[:, 0:128], cen[:, 0:128], MUL, [cenr], [t19r])
                op("dve", lambda e, t19=t19, st_=st_: e.tensor_reduce(out=st_[:, 2:4], in_=t19[:, 0:128].rearrange("p (h q) -> p h q", h=2),
                                                                       axis=AX.X, op=ADD), reads=[t19r], writes=[str_])
                ACT(st_[:, 2:4], st_[:, 2:4], AF.Sqrt, [str_], [str_], scale=1.0 / 64, bias=GN_EPS)
                op("dve", lambda e, st_=st_: e.reciprocal(out=st_[:, 2:4], in_=st_[:, 2:4]), reads=[str_], writes=[str_])
                TTo("dve", c3, c3, st_[:, 2:4].unsqueeze(2).to_broadcast([128, 2, 64]), MUL, [cenr, str_], [cenr])
                pt, ptr = PSI(6)
                TR(pt[:, 0:128], cen[:, 0:128], [cenr], [ptr])
                ACT(t19[:, 0:128], pt[:, 0:128], AF.Identity, [ptr, RP], [t19r], scale=pcol("lnw", hh), bias=pcol("lnb", hh))
                TTo("dve", t19[:, 0:128], t19[:, 0:128], bon[:, cs], ADD, [t19r, bonr], [t19r])
                TTo("dve", _r(ymix[:, hh, cs]), t19[:, 0:128], gg[:, cs], MUL, [t19r, ggr], [R("ymix", hh)])
            if smp:
                for s in range(NSEG):
                    pt, ptr = PSI(6 + s % 2)
                    TR(pt[0:64, 0:128], S0T[:, s, :], [R("S0T", s)], [ptr])
                    so, sor = sout[s % 2], R("sout", s % 2)
                    CP("act", so[0:64, :], pt[0:64, 0:128], [ptr], [sor])
                    kb.dma("pool", wkv_s[s, 2 * hh:2 * hh + 2].rearrange("h v k -> v h k"), so[0:64, :].rearrange("p (h k) -> p h k", h=2),
                           reads=[sor])
        if smp:
            for b in range(4):
                nb = min(16, 50 - 16 * b)
                for ch in range(nb):
                    pp, pr = PSI(6 + ch % 2)
                    TR(pp[0:16, 0:128], SHO[:, 16 * b + ch, :], [R("SHO")], [pr])
                    CP("act", stg[0:16, ch * 128:(ch + 1) * 128], pp[0:16, 0:128], [pr], [R("stg")])
                kb.dma("pool", shift_s[:, b * 2048:b * 2048 + nb * 128], stg[0:16, 0:nb * 128], reads=[R("stg")])

    def final_states_b():
        stg = scr[:, 0:D]
        for hh in range(16):
            pt, ptr = PSI(6 + hh % 2)
            TR(pt[0:64, 0:128], S0T[:, hh, :], [R("S0T", hh)], [ptr])
            so, sor = sout[hh % 2], R("sout", hh % 2)
            CP("act", so[0:64, :], pt[0:64, 0:128], [ptr], [sor])
            kb.dma("pool", wkv_p[2 * hh:2 * hh + 2].rearrange("h v k -> v h k"), so[0:64, :].rearrange("p (h k) -> p h k", h=2), reads=[sor])
        pp, pr = PSI(6)
        TR(pp[0:50, 0:128], shift_tail[:], [R("shift_tail")], [pr])
        CP("act", stg[0:50, 0:128], pp[0:50, 0:128], [pr], [R("stg")])
        kb.dma("pool", shift_p.rearrange("(k p) -> k p", p=128), stg[0:50, 0:128], reads=[R("stg")])

    tiles = cfg["tiles"]
    for ti, tile in enumerate(tiles):
        if tile[0] == "p":
            t0 = tile[1]
            T = 256
            load_x(xp[t0:t0 + T, :], T)
            dst = yp[t0:t0 + T, :]
            wins = [dict(c0=0, T=256, nseg=1, L=256, kind="p")]
        else:
            T = 128
            load_x(xs, T)
            dst = ys
            wins = [dict(c0=0, T=128, nseg=16, L=8, kind="s")]
        for st in stages:
            kb.barrier()
            if st.startswith("ffn"):
                ffn(T, int(st[3]), int(st[4]))
            elif st == "mixc":
                rmsnorm(T, "ng", (1 * 3 + 1) * KD)
                for w in wins:
                    mixer_c(w)
                    kb.barrier()
            elif st == "mixab":
                rmsnorm(T, "ng", (0 * 3 + 1) * KD)
                for w in wins:
                    mixer_ab(w)
                    kb.barrier()
        kb.barrier()
        store_y(dst, T)
        last_prompt = tile[0] == "p" and (ti + 1 == len(tiles) or tiles[ti + 1][0] != "p")
        if last_prompt:
            kb.barrier()
            if use_c:
                final_states_c(lru_p, convc_p)
            if use_ab:
                final_states_ab()

    kb.finish()


ALL_STAGES = ["ffn00", "mixab", "ffn01", "ffn10", "mixc", "ffn11"]
FULL_CFG = {"tiles": [("p", 256 * i) for i in range(8)] + [("s",)], "stages": ALL_STAGES}


def make_in_maps(inp, cfg):
    f = lambda a: np.ascontiguousarray(np.asarray(a, dtype=np.float32))
    stages = cfg["stages"]
    use_ffn = any(s.startswith("ffn") for s in stages)
    use_ab = "mixab" in stages
    use_c = "mixc" in stages
    shared = {"norm_gain": f(inp["norm_gain"]), "final_norm_gain": f(inp["final_norm_gain"])}
    if use_ffn:
        shared["w_ffn_in"] = f(inp["w_ffn_in"])
        shared["w_ffn_out"] = f(inp["w_ffn_out"])
    if use_c:
        for k in ("w_in_c", "w_out_c", "conv_w_c", "conv_b_c", "w_gate_a_c", "w_gate_x_c", "b_gate_a_c", "b_gate_x_c", "lambda_c"):
            shared[k] = f(inp[k][0])
    if use_ab:
        for k in ("w_in_ab", "w_out_ab", "conv_w_a", "conv_b_a", "dt_bias_a", "a_log_a", "d_skip_a", "gnorm_a", "mu_b", "w0_b", "w2_b",
                  "a0_b", "a2_b", "g2_b", "k_k_b", "k_a_b", "ln_w_b", "ln_b_b"):
            shared[k] = f(inp[k][0])
        shared["r_k_b"] = f(inp["r_k_b"][0]).reshape(D)
    in_maps = []
    for c in range(8):
        m = dict(shared)
        m["xp"] = f(inp["x_prompt"][c % 4])
        sl = slice(16 * c, 16 * c + 16)
        m["xs"] = f(inp["x_sample"][sl]).reshape(128, D)
        if use_c:
            m["st_lru"] = f(inp["state_lru_c"][0, sl])
            m["st_convc"] = f(inp["state_conv_c"][0, sl]).reshape(48, D)
        if use_ab:
            m["st_ssm"] = f(inp["state_ssm_a"][0, sl])
            m["st_conva"] = f(inp["state_conv_a"][0, sl]).reshape(48, 3072)
            m["st_wkv"] = f(inp["state_wkv_b"][0, sl])
            m["st_shift"] = f(inp["state_shift_b"][0, sl]).reshape(16, 6400)
        in_maps.append(m)
    return in_maps


def gather(r, cfg):
    stages = cfg["stages"]
    use_ab = "mixab" in stages
    use_c = "mixc" in stages
    out = {}
    out["y_prompt"] = np.stack([r[c]["yp"] for c in range(4)], 0)
    out["y_sample"] = np.concatenate([r[c]["ys"].reshape(16, 8, D) for c in range(8)], 0)
    if use_ab:
        out["ssm_p"] = np.stack([r[c]["ssm_p"] for c in range(4)], 0)[None]
        out["conva_p"] = np.stack([r[c]["conva_p"] for c in range(4)], 0)[None]
        out["wkv_p"] = np.stack([r[c]["wkv_p"] for c in range(4)], 0)[None]
        out["shift_p"] = np.stack([r[c]["shift_p"].reshape(1, 6400) for c in range(4)], 0)[None]
        out["ssm_s"] = np.concatenate([r[c]["ssm_s"] for c in range(8)], 0)[None]
        out["conva_s"] = np.concatenate([r[c]["conva_s"].reshape(16, 3, 3072) for c in range(8)], 0)[None]
        out["wkv_s"] = np.concatenate([r[c]["wkv_s"] for c in range(8)], 0)[None]
        out["shift_s"] = np.concatenate([r[c]["shift_s"].reshape(16, 1, 6400) for c in range(8)], 0)[None]
    if use_c:
        out["lru_p"] = np.stack([r[c]["lru_p"] for c in range(4)], 0)[None]
        out["convc_p"] = np.stack([r[c]["convc_p"] for c in range(4)], 0)[None]
        out["lru_s"] = np.concatenate([r[c]["lru_s"] for c in range(8)], 0)[None]
        out["convc_s"] = np.concatenate([r[c]["convc_s"].reshape(16, 3, D) for c in range(8)], 0)[None]
    return out


def kernel(**inp):
    cfg = FULL_CFG
    nc, kb = build(cfg)
    in_maps = make_in_maps(inp, cfg)
    res = run_bass_kernel_spmd(nc, in_maps, core_ids=list(range(8)))
    o = gather(res.results, cfg)
    return (o["y_prompt"], o["y_sample"], o["ssm_p"], o["conva_p"], o["wkv_p"], o["shift_p"], o["lru_p"], o["convc_p"],
            o["ssm_s"], o["conva_s"], o["wkv_s"], o["shift_s"], o["lru_s"], o["convc_s"])
```

```python
import numpy as np
from contextlib import ExitStack
import concourse.bass as bass
import concourse.mybir as mybir
from concourse.bass_utils import run_bass_kernel_spmd

F32 = mybir.dt.float32
F32R = mybir.dt.float32r
AF = mybir.ActivationFunctionType
ALU = mybir.AluOpType
AX = mybir.AxisListType
MUL, ADD, SUB, MAX, MIN = ALU.mult, ALU.add, ALU.subtract, ALU.max, ALU.min

D = 2048
KD = 16
DFF = 5504
NF = 43
SEQ = 2048
NSEG = 16
LSEG = 8
EPS = 1e-6
GN_EPS = 64e-5
NSLOT = 3
SLOT = 4096
NDS = 12
IN_AB = 11552
COL_Z, COL_XS, COL_B, COL_C, COL_DT = 0, 2048, 4096, 4608, 5120
COL_R, COL_K, COL_V, COL_LW, COL_XG = 5152, 7200, 9248, 11296, 11424
NTP = 20
NTQ = 21
TPW = 260


class Tok:
    __slots__ = ("sem", "val")

    def __init__(self, sem, val):
        self.sem = sem
        self.val = val


class Region:
    __slots__ = ("name", "w", "r")

    def __init__(self, name):
        self.name = name
        self.w = None
        self.r = {}


class _FakeSem:
    pass


class KB:
    def __init__(self, nc, es, mode, sched, need_inc):
        self.nc = nc
        self.mode = mode
        self.dry = mode == "collect"
        self.real = mode == "real"
        self.sched = sched
        self.need_inc = need_inc if need_inc is not None else set()
        self.req = []
        self.eng = {"pe": nc.tensor, "act": nc.scalar, "dve": nc.vector, "pool": nc.gpsimd, "sp": nc.sync}
        self.sem = {}
        self.semname = {}
        self.cnt = {}
        self.rank = {}
        self.last = {}
        self.waited = {}
        self.dsem = {}
        self.dn = {}
        self.dhist = {}
        self.regs = {}
        self.all_dma = []
        self.n_ins = 0
        if mode != "collect":
            for e in self.eng:
                self.sem[e] = es.enter_context(nc.semaphore("s_" + e)) if self.real else _FakeSem()
                self.semname[id(self.sem[e])] = e
                self.cnt[e] = 0
                self.rank[e] = 0
                self.last[e] = None
            for q in ("sp", "pool"):
                self.dsem[q] = [es.enter_context(nc.semaphore("d_%s_%d" % (q, i))) if self.real else _FakeSem() for i in range(NDS)]
                self.dn[q] = 0
                self.dhist[q] = [None] * NDS
        self.w_issued = 0
        self.w_next = 0
        self.wslots = None

    def R(self, *key):
        r = self.regs.get(key)
        if r is None:
            r = Region(key)
            self.regs[key] = r
        return r

    def _need(self, e, tok, acc):
        if tok is None:
            return
        if e == "pe" and tok.sem is self.sem["pe"]:
            return
        key = (e, id(tok.sem))
        if self.waited.get(key, 0) >= tok.val:
            return
        self.waited[key] = tok.val
        acc[id(tok.sem)] = tok
        if self.mode == "sim":
            en = self.semname.get(id(tok.sem))
            if en is not None:
                self.need_inc.add((en, tok.val))

    def _emit_wait(self, e, t):
        if self.real:
            self.eng[e].wait_ge(t.sem, t.val)
        self.n_ins += 1

    def _wait(self, e, tok):
        acc = {}
        self._need(e, tok, acc)
        for t in acc.values():
            self._emit_wait(e, t)

    def _collect(self, e, reads, writes):
        acc = {}
        for R in reads:
            self._need(e, R.w, acc)
        for R in writes:
            self._need(e, R.w, acc)
            for t in R.r.values():
                self._need(e, t, acc)
        return list(acc.values())

    def _commit(self, tok, reads, writes):
        for R in reads:
            R.r[id(tok.sem)] = tok
        for R in writes:
            R.w = tok
            R.r = {}

    def op(self, e, fn, reads=(), writes=()):
        if self.dry:
            return None
        toks = self._collect(e, reads, writes)
        for t in toks[:-1]:
            self._emit_wait(e, t)
        self.cnt[e] += 1
        idx = self.cnt[e]
        if self.real:
            ins = fn(self.eng[e])
            if toks:
                ins._wait_ge(toks[-1].sem, toks[-1].val)
            if (e, idx) in self.need_inc:
                self.rank[e] += 1
                ins.then_inc(self.sem[e], 1)
                tok = Tok(self.sem[e], self.rank[e])
            else:
                tok = Tok(self.sem[e], self.rank[e])
        else:
            tok = Tok(self.sem[e], idx)
        self.n_ins += 1
        self.last[e] = (idx, tok)
        self._commit(tok, reads, writes)
        return tok

    def dma(self, q, out, in_, reads=(), writes=(), r32=False, slow=False):
        if self.dry:
            return None
        for t in self._collect(q, reads, writes):
            self._emit_wait(q, t)
        n = self.dn[q]
        slot = n % NDS
        self._wait(q, self.dhist[q][slot])
        sem = self.dsem[q][slot]
        tok = Tok(sem, 16 * (n // NDS + 1))
        if self.real:
            if r32:
                self.nc.dge_precook = False
                ins = self.eng[q].dma_start(out=out.bitcast(F32R), in_=in_.bitcast(F32R))
                self.nc.dge_precook = True
            elif slow:
                ins = self.eng[q].dma_start(out=out, in_=in_, allow_slow_non_contiguous=True)
            else:
                ins = self.eng[q].dma_start(out=out, in_=in_)
            ins.then_inc(sem, 16)
        self.n_ins += 1
        self.dn[q] = n + 1
        self.dhist[q][slot] = tok
        self._commit(tok, reads, writes)
        self.all_dma.append(tok)
        return tok

    def barrier(self):
        if self.dry:
            return
        ces = ("pe", "act", "dve", "pool")
        for e in ces:
            for f in ces:
                if f != e and self.last[f] is not None:
                    idx, tok = self.last[f]
                    if self.real and (f, idx) not in self.need_inc:
                        continue
                    self._wait(e, tok)
            for t in self.dhist["pool"]:
                self._wait(e, t)

    def finish(self):
        if self.dry:
            return
        for tok in self.all_dma:
            self._wait("sp", tok)

    def wnext(self, spec):
        i = self.w_next
        self.w_next += 1
        s = i % NSLOT
        if self.dry:
            self.req.append(spec)
            return self.wslots[s], self.R("wslot", s)
        assert self.sched[i] == spec, (i, self.sched[i], spec)
        hi = min(i + NSLOT, len(self.sched))
        while self.w_issued < hi:
            j = self.w_issued
            sj = j % NSLOT
            for (dst, src) in self.wsrc(self.sched[j], self.wslots[sj]):
                self.dma("sp", dst, src, writes=[self.R("wslot", sj)], r32=True)
            self.w_issued += 1
        return self.wslots[s], self.R("wslot", s)


def _r(ap):
    return ap.bitcast(F32R)


def build(cfg):
    sched = None
    need = None
    for mode in ("collect", "sim", "real"):
        nc = bass.Bass("TRN2", target_bir_lowering=False)
        with ExitStack() as es:
            kb = KB(nc, es, mode, sched, need)
            _program(nc, es, kb, cfg)
        if mode == "collect":
            sched = kb.req
        elif mode == "sim":
            need = kb.need_inc
    return nc, kb


def _program(nc, es, kb, cfg):
    stages = cfg["stages"]
    use_ffn = any(s.startswith("ffn") for s in stages)
    use_ab = "mixab" in stages
    use_c = "mixc" in stages

    def din(name, shape):
        return nc.dram_tensor(name, list(shape), F32, kind="ExternalInput").ap()

    def dout(name, shape):
        return nc.dram_tensor(name, list(shape), F32, kind="ExternalOutput").ap()

    xp = din("xp", [SEQ, D])
    xs = din("xs", [128, D])
    norm_gain = din("norm_gain", [2, 3, D])
    final_gain = din("final_norm_gain", [D])
    yp = dout("yp", [SEQ, D])
    ys = dout("ys", [128, D])
    if use_ffn:
        w_ffn_in = din("w_ffn_in", [2, 2, D, 2 * DFF])
        w_ffn_out = din("w_ffn_out", [2, 2, DFF, D])
    if use_c:
        w_in_c = din("w_in_c", [D, 2 * D])
        w_out_c = din("w_out_c", [D, D])
        conv_w_c = din("conv_w_c", [4, D])
        conv_b_c = din("conv_b_c", [D])
        w_ga = din("w_gate_a_c", [8, 256, 256])
        w_gx = din("w_gate_x_c", [8, 256, 256])
        b_ga = din("b_gate_a_c", [D])
        b_gx = din("b_gate_x_c", [D])
        lam = din("lambda_c", [D])
        st_lru = din("st_lru", [NSEG, D])
        st_convc = din("st_convc", [NSEG * 3, D])
        lru_p = dout("lru_p", [D])
        convc_p = dout("convc_p", [3, D])
        lru_s = dout("lru_s", [NSEG, D])
        convc_s = dout("convc_s", [NSEG * 3, D])
    if use_ab:
        w_in_ab = din("w_in_ab", [D, IN_AB])
        w_out_ab = din("w_out_ab", [2 * D, D])
        conv_w_a = din("conv_w_a", [4, 3072])
        conv_b_a = din("conv_b_a", [3072])
        dt_bias = din("dt_bias_a", [32])
        a_log = din("a_log_a", [32])
        d_skip = din("d_skip_a", [32])
        gnorm = din("gnorm_a", [D])
        mu_b = din("mu_b", [6400])
        w0_b = din("w0_b", [D])
        w2_b = din("w2_b", [64, D])
        a0_b = din("a0_b", [D])
        a2_b = din("a2_b", [64, D])
        g2_b = din("g2_b", [128, D])
        k_k_b = din("k_k_b", [D])
        k_a_b = din("k_a_b", [D])
        r_k_b = din("r_k_b", [D])
        ln_w_b = din("ln_w_b", [D])
        ln_b_b = din("ln_b_b", [D])
        st_ssm = din("st_ssm", [NSEG, 32, 64, 128])
        st_conva = din("st_conva", [NSEG * 3, 3072])
        st_wkv = din("st_wkv", [NSEG, 32, 64, 64])
        st_shift = din("st_shift", [NSEG, 6400])
        ssm_p = dout("ssm_p", [32, 64, 128])
        conva_p = dout("conva_p", [3, 3072])
        wkv_p = dout("wkv_p", [32, 64, 64])
        shift_p = dout("shift_p", [6400])
        ssm_s = dout("ssm_s", [NSEG, 32, 64, 128])
        conva_s = dout("conva_s", [NSEG * 3, 3072])
        wkv_s = dout("wkv_s", [NSEG, 32, 64, 64])
        shift_s = dout("shift_s", [NSEG, 6400])

    def sb(name, shape, dt=F32):
        return es.enter_context(nc.sbuf_tensor(name, list(shape), dt))

    TT = 256
    x = sb("x", [128, KD, TT])
    hT = sb("hT", [128, KD, TT])
    hid = sb("hid", [128, 28, TT])
    hidf = hid[:].rearrange("p a b -> p (a b)")
    scr = sb("scr", [128, 5952])
    kb.wslots = [sb("wslot%d" % i, [128, SLOT]) for i in range(NSLOT)]
    ident = sb("ident", [128, 128])
    ones = sb("ones", [128, 128])
    bones = sb("bones", [128, 128])
    params = sb("params", [128, 6 * 128])
    tp = [sb("tp%d" % i, [128, TPW]) for i in range(NTP)]
    tq = [sb("tq%d" % i, [128, TPW]) for i in range(NTQ)]
    ps = [es.enter_context(nc.psum_tensor("ps%d" % i, [128, 512], F32)) for i in range(8)]
    m_sl = [sb("m_sl%d" % i, [128, 128]) for i in range(2)]
    m_iu = [sb("m_iu%d" % i, [128, 128]) for i in range(2)]
    m_neg = [sb("m_neg%d" % i, [128, 128]) for i in range(2)]
    m_pair = [sb("m_pair%d" % i, [128, 256]) for i in range(2)]
    m_reset = [sb("m_reset%d" % i, [128, 256]) for i in range(2)]
    segmask = sb("segmask", [128, 16])
    bsel = sb("bsel", [16, 128])
    es.enter_context(nc.Block())

    R = kb.R
    op = kb.op

    def MM(out, lhsT, rhs, start, stop, rd, wr):
        op("pe", lambda e: e.matmul(out, lhsT, rhs, start=start, stop=stop), reads=rd, writes=wr)

    def TR(out, in_, rd, wr):
        p = in_.shape[0]
        op("pe", lambda e: e.transpose(out, in_, ident[:p, :p]), reads=list(rd) + [R("ident")], writes=wr)

    def TTo(e, out, in0, in1, o, rd, wr):
        op(e, lambda g: g.tensor_tensor(out=out, in0=in0, in1=in1, op=o), reads=rd, writes=wr)

    def TS(e, out, in0, s1, s2, o0, o1, rd, wr):
        if o1 is None:
            op(e, lambda g: g.tensor_scalar(out=out, in0=in0, scalar1=s1, scalar2=None, op0=o0), reads=rd, writes=wr)
        else:
            op(e, lambda g: g.tensor_scalar(out=out, in0=in0, scalar1=s1, scalar2=s2, op0=o0, op1=o1), reads=rd, writes=wr)

    def STT(out, in0, scalar, in1, o0, o1, rd, wr):
        op("dve", lambda g: g.scalar_tensor_tensor(out=out, in0=in0, scalar=scalar, in1=in1, op0=o0, op1=o1), reads=rd, writes=wr)

    def ACT(out, in_, func, rd, wr, bias=None, scale=None):
        kw = {}
        if bias is not None:
            kw["bias"] = bias
        if scale is not None:
            kw["scale"] = scale
        op("act", lambda g: g.activation(out=out, in_=in_, func=func, **kw), reads=rd, writes=wr)

    def CP(e, out, in_, rd, wr):
        if e == "act":
            op("act", lambda g: g.activation(out=out, in_=in_, func=AF.Copy), reads=rd, writes=wr)
        else:
            op(e, lambda g: g.tensor_copy(out=out, in_=in_), reads=rd, writes=wr)

    def MS(e, ap, val, wr):
        op(e, lambda g: g.memset(ap, val), writes=wr)

    def v3(ap, n):
        return ap.rearrange("p (s l) -> p s l", s=n)

    pctr = [0]

    def PS(lo=0, hi=6):
        i = lo + pctr[0] % (hi - lo)
        pctr[0] += 1
        return ps[i], R("ps", i)

    def T_(i):
        return tp[i], R("tp", i)

    def Q_(i):
        return tq[i], R("tq", i)

    def wsrc(spec, slot):
        kind = spec[0]
        if kind == "ffn_in":
            _, li, fi, j = spec
            W = w_ffn_in[li, fi].rearrange("(k p) c -> p k c", p=128)
            dst = slot[:].rearrange("p (g k c) -> p g k c", g=2, k=KD)
            return [(dst[:, g], W[:, :, g * DFF + j * 128: g * DFF + (j + 1) * 128]) for g in range(2)]
        if kind == "ffn_out":
            _, li, fi, j0, j1, m = spec
            W = w_ffn_out[li, fi].rearrange("(j p) c -> p j c", p=128)
            nf = j1 - j0
            dst = slot[:, :nf * 128].rearrange("p (j c) -> p j c", j=nf)
            return [(dst, W[:, j0:j1, m * 128:(m + 1) * 128])]
        if kind == "ffn_in2":
            _, li, fi, g, j, n = spec
            W = w_ffn_in[li, fi].rearrange("(k p) c -> p k c", p=128)
            dst = slot[:, :KD * n * 128].rearrange("p (k c) -> p k c", k=KD)
            return [(dst, W[:, :, g * DFF + j * 128: g * DFF + (j + n) * 128])]
        if kind == "ffn_out2":
            _, li, fi, ja, jb, mm = spec
            W = w_ffn_out[li, fi].rearrange("(j p) c -> p j c", p=128)
            nfb = jb - ja
            dst = slot[:, :nfb * 256].rearrange("p (j c) -> p j c", j=nfb)
            return [(dst, W[:, ja:jb, mm * 256:(mm + 1) * 256])]
        if kind == "cols":
            _, wname, row0, pieces = spec
            Wd = {"w_in_c": w_in_c if use_c else None, "w_out_c": w_out_c if use_c else None,
                  "w_in_ab": w_in_ab if use_ab else None, "w_out_ab": w_out_ab if use_ab else None}[wname]
            W = Wd[row0:row0 + D, :].rearrange("(k p) c -> p k c", p=128)
            tot = sum(n for (_, n) in pieces)
            dst = slot[:, :KD * tot].rearrange("p (k c) -> p k c", k=KD)
            out = []
            o = 0
            for (c0, n) in pieces:
                out.append((dst[:, :, o:o + n], W[:, :, c0:c0 + n]))
                o += n
            return out
        if kind == "gates":
            _, h = spec
            dst = slot[:, :1024].rearrange("p (g i c) -> p g i c", g=2, i=2)
            return [(dst[:, 0], w_ga[h].rearrange("(i p) c -> p i c", p=128)),
                    (dst[:, 1], w_gx[h].rearrange("(i p) c -> p i c", p=128))]
        if kind == "lora":
            _, hh = spec
            W = w_in_ab.rearrange("(k p) c -> p k c", p=128)
            dst = slot[:, :KD * 128].rearrange("p (k c) -> p k c", k=KD)
            o = KD * 128
            return [(dst, W[:, :, COL_V + hh * 128: COL_V + (hh + 1) * 128]),
                    (slot[0:64, o:o + 128], w2_b[:, hh * 128:(hh + 1) * 128]),
                    (slot[64:128, o:o + 128], a2_b[:, hh * 128:(hh + 1) * 128]),
                    (slot[:, o + 128:o + 256], g2_b[:, hh * 128:(hh + 1) * 128])]
        if kind == "wdt":
            W = w_in_ab.rearrange("(k p) c -> p k c", p=128)
            dst = slot[:, :KD * 32].rearrange("p (k c) -> p k c", k=KD)
            return [(dst, W[:, :, COL_DT:COL_DT + 32])]
        raise ValueError(spec)

    kb.wsrc = wsrc

    MS("pool", ones[:], 1.0, [R("ones")])
    MS("pool", ident[:], 1.0, [R("ident")])
    op("pool", lambda e: e.affine_select(out=_r(ident[:]), in_=ident[:], pattern=[[-1, 128]],
                                         compare_op=ALU.is_equal, fill=0.0, base=0, channel_multiplier=1),
       reads=[R("ident")], writes=[R("ident")])
    MS("pool", bones[:], 0.0, [R("bones")])
    MS("pool", bones[0:64, 0:64], 1.0, [R("bones")])
    MS("pool", bones[64:128, 64:128], 1.0, [R("bones")])
    MS("pool", bsel[:], 1.0, [R("bsel")])
    op("pool", lambda e: e.affine_select(out=bsel[:], in_=bsel[:], pattern=[[1, 128]], compare_op=ALU.is_ge, fill=0.0,
                                         base=0, channel_multiplier=-8), reads=[R("bsel")], writes=[R("bsel")])
    op("pool", lambda e: e.affine_select(out=bsel[:], in_=bsel[:], pattern=[[-1, 128]], compare_op=ALU.is_ge, fill=0.0,
                                         base=7, channel_multiplier=8), reads=[R("bsel")], writes=[R("bsel")])
    MS("pool", m_sl[0][:], 1.0, [R("m_sl", 0)])
    op("pool", lambda e: e.affine_select(out=m_sl[0][:], in_=m_sl[0][:], pattern=[[-1, 128]], compare_op=ALU.is_gt, fill=0.0,
                                         base=0, channel_multiplier=1), reads=[R("m_sl", 0)], writes=[R("m_sl", 0)])
    MS("pool", m_iu[0][:], 1.0, [R("m_iu", 0)])
    op("pool", lambda e: e.affine_select(out=m_iu[0][:], in_=m_iu[0][:], pattern=[[1, 128]], compare_op=ALU.is_ge, fill=0.0,
                                         base=0, channel_multiplier=-1), reads=[R("m_iu", 0)], writes=[R("m_iu", 0)])
    pseg, rseg = ps[6], R("ps", 6)
    MM(pseg[:, 0:128], bsel[:], bsel[:], True, True, [R("bsel")], [rseg])
    TTo("dve", m_sl[1][:], m_sl[0][:], pseg[:, 0:128], MUL, [R("m_sl", 0), rseg], [R("m_sl", 1)])
    TTo("dve", m_iu[1][:], m_iu[0][:], pseg[:, 0:128], MUL, [R("m_iu", 0), rseg], [R("m_iu", 1)])
    TR(ps[7][:, 0:16], bsel[:], [R("bsel")], [R("ps", 7)])
    CP("dve", segmask[:], ps[7][:, 0:16], [R("ps", 7)], [R("segmask")])
    for i in range(2):
        TS("dve", m_neg[i][:], m_iu[i][:], 30000.0, -30000.0, MUL, ADD, [R("m_iu", i)], [R("m_neg", i)])
        CP("dve", _r(m_pair[i][:, 128:256]), m_iu[i][:], [R("m_iu", i)], [R("m_pair", i)])
        TTo("dve", _r(m_pair[i][:, 0:128]), m_iu[i][:], ident[:], SUB, [R("m_iu", i), R("ident")], [R("m_pair", i)])
        MS("dve", m_reset[i][:], 1.0, [R("m_reset", i)])
    MS("dve", m_reset[0][:].rearrange("p (c t) -> p c t", t=128)[:, :, 0:1], 0.0, [R("m_reset", 0)])
    MS("dve", m_reset[1][:].rearrange("p (c t) -> p c t", t=8)[:, :, 0:1], 0.0, [R("m_reset", 1)])

    pcol_index = {}
    pst_r = R("scr_stage")
    pstage = scr[:, 2048:2816].rearrange("p (g c) -> p g c", g=6)
    MS("pool", scr[:, 2048:2816], 0.0, [pst_r])
    cursor = [0, 0]

    def stage(name, ap2d):
        n = ap2d.shape[0]
        if cursor[1] + n > 128:
            cursor[0] += 1
            cursor[1] = 0
        g, r0 = cursor
        kb.dma("pool", pstage[r0:r0 + n, g, :], ap2d, writes=[pst_r])
        for i in range(n):
            pcol_index[(name, i)] = g * 128 + r0 + i
        cursor[1] += n

    def v2(ap):
        return ap.rearrange("(k p) -> k p", p=128)

    stage("ng", norm_gain.rearrange("a b (k p) -> (a b k) p", p=128))
    stage("fng", v2(final_gain))
    if use_c:
        stage("cwc", conv_w_c.rearrange("a (k p) -> (a k) p", p=128))
        stage("cbc", v2(conv_b_c))
        stage("bga", v2(b_ga))
        stage("bgx", v2(b_gx))
        stage("lam", v2(lam))
    if use_ab:
        stage("cwa", conv_w_a.rearrange("a (k p) -> (a k) p", p=128))
        stage("cba", v2(conv_b_a))
        stage("gna", v2(gnorm))
        stage("mu", v2(mu_b))
        for nm, t in (("w0", w0_b), ("a0", a0_b), ("kk", k_k_b), ("ka", k_a_b), ("rk", r_k_b), ("lnw", ln_w_b), ("lnb", ln_b_b)):
            stage(nm, v2(t))
    ngrp = cursor[0] + 1
    assert ngrp <= 6
    for g in range(ngrp):
        pt, pr = (ps[6], R("ps", 6)) if g < 4 else (ps[7], R("ps", 7))
        TR(pt[:, (g % 4) * 128:(g % 4 + 1) * 128], pstage[:, g, :], [pst_r], [pr])
    CP("dve", params[:, 0:512], ps[6][:, :], [R("ps", 6)], [R("params")])
    if ngrp > 4:
        CP("dve", params[:, 512:768], ps[7][:, 0:256], [R("ps", 7)], [R("params")])
    RP = R("params")

    def pcol(name, i):
        c = pcol_index[(name, i)]
        return params[:, c:c + 1]

    dcols = sb("dcols", [128, 80])
    RD = R("dcols")
    if use_c:
        l0 = pcol_index[("lam", 0)]
        ACT(dcols[:, 0:16], params[:, l0:l0 + 16], AF.Exp, [RP], [RD], scale=-1.0)
        ACT(dcols[:, 0:16], dcols[:, 0:16], AF.Ln, [RD], [RD], bias=1.0)
        TS("dve", dcols[:, 0:16], dcols[:, 0:16], -8.0, None, MUL, None, [RD], [RD])
    if use_ab:
        w0i = pcol_index[("w0", 0)]
        kai = pcol_index[("ka", 0)]
        TS("dve", dcols[:, 16:32], params[:, w0i:w0i + 16], -1.0, None, MUL, None, [RP], [RD])
        TS("dve", dcols[:, 32:48], params[:, kai:kai + 16], -1.0, 1.0, MUL, ADD, [RP], [RD])
        dsv = d_skip.rearrange("(k t) -> t k", t=2)
        kb.dma("pool", dcols[0:64, 48:64], dsv[0, :].partition_broadcast(64), writes=[RD], slow=True)
        kb.dma("pool", dcols[64:128, 48:64], dsv[1, :].partition_broadcast(64), writes=[RD], slow=True)
        bcast = sb("bcast", [128, 64])
        kb.dma("pool", bcast[:, 0:32], dt_bias.partition_broadcast(128), writes=[R("bcast")])
        kb.dma("pool", bcast[:, 32:64], a_log.partition_broadcast(128), writes=[R("bcast")])
        ACT(bcast[:, 32:64], bcast[:, 32:64], AF.Exp, [R("bcast")], [R("bcast")])
        TS("dve", bcast[:, 32:64], bcast[:, 32:64], -1.0, None, MUL, None, [R("bcast")], [R("bcast")])
        wdt = sb("wdt", [128, KD, 32])
        kb.dma("sp", wdt[:], w_in_ab.rearrange("(k p) c -> p k c", p=128)[:, :, COL_DT:COL_DT + 32], writes=[R("wdt")], r32=True)

    def dcol(base, i):
        return dcols[:, base + i:base + i + 1]

    if use_c:
        lru_st = sb("lru_st", [128, 16])
        convc_tail = sb("convc_tail", [128, 16, 3])
        MS("dve", lru_st[:], 0.0, [R("lru_st")])
        MS("dve", convc_tail[:], 0.0, [R("convc_tail")])
    if use_ab:
        S_T = sb("S_T", [128, 2048])
        S0T = sb("S0T", [128, 16, 64])
        conva_tail = sb("conva_tail", [128, 24, 3])
        shift_tail = sb("shift_tail", [128, 50])
        MS("dve", S_T[:], 0.0, [R("S_T", i) for i in range(16)])
        MS("dve", S0T[:], 0.0, [R("S0T", i) for i in range(16)])
        MS("dve", conva_tail[:], 0.0, [R("conva_tail")])
        MS("dve", shift_tail[:], 0.0, [R("shift_tail")])

    xstage = scr[:, 0:D]
    RXS = R("xstage")

    def load_x(src, T):
        for tc in range(T // 128):
            kb.dma("pool", xstage, src[tc * 128:(tc + 1) * 128, :], writes=[RXS])
            for kk in range(4):
                pt, pr = ps[6 + (kk % 2)], R("ps", 6 + (kk % 2))
                for q in range(4):
                    k = kk * 4 + q
                    TR(pt[:, q * 128:(q + 1) * 128], xstage[:, k * 128:(k + 1) * 128], [RXS], [pr])
                CP("dve", x[:, kk * 4:kk * 4 + 4, tc * 128:(tc + 1) * 128], pt[:].rearrange("p (a b) -> p a b", a=4),
                   [pr], [R("x", kk * 4 + q) for q in range(4)])

    def rmsnorm(T, gname, gbase, out_r32=True):
        rstd, rr = T_(0)
        for k in range(KD):
            s, sr = Q_(k % 2)
            ACT(_r(s[:, :T]), x[:, k, :T], AF.Square, [R("x", k)], [sr])
            MM(ps[7][:, :T], _r(ones[:]), _r(s[:, :T]), k == 0, k == KD - 1, [sr, R("ones")], [R("ps", 7)])
        ACT(rstd[:, :T], ps[7][:, :T], AF.Sqrt, [R("ps", 7)], [rr], scale=1.0 / D, bias=EPS)
        op("dve", lambda e: e.reciprocal(out=rstd[:, :T], in_=rstd[:, :T]), reads=[rr], writes=[rr])
        for k in range(KD):
            if out_r32:
                o, wr = _r(hT[:, k, :T]), R("hT", k)
            else:
                o, wr = x[:, k, :T], R("x", k)
            STT(o, x[:, k, :T], pcol(gname, gbase + k), rstd[:, :T], MUL, MUL, [R("x", k), rr, RP], [wr])

    ffn_ctr = [0]

    def ffn(T, li, fi):
        rmsnorm(T, "ng", (li * 3 + (0 if fi == 0 else 2)) * KD)
        for (j0, j1) in ((0, 22), (22, NF)):
            j = j0
            while j < j1:
                n = min(2, j1 - j)
                c = ffn_ctr[0]
                ffn_ctr[0] += 1
                base = (c % 2) * 4
                for g in range(2):
                    slot, sreg = kb.wnext(("ffn_in2", li, fi, g, j, n))
                    wv = slot[:, :KD * n * 128].rearrange("p (k c) -> p k c", k=KD)
                    for q in range(n):
                        pp, rr = ps[base + g * 2 + q], R("ps", base + g * 2 + q)
                        for k in range(KD):
                            MM(pp[:, :T], _r(wv[:, k, q * 128:(q + 1) * 128]), _r(hT[:, k, :T]), k == 0, k == KD - 1, [sreg, R("hT", k)], [rr])
                for q in range(n):
                    pg, rg = ps[base + q], R("ps", base + q)
                    pu, ru = ps[base + 2 + q], R("ps", base + 2 + q)
                    st, str_ = T_(1 + q)
                    ACT(st[:, :T], pg[:, :T], AF.Silu, [rg], [str_])
                    TTo("dve", _r(hid[:, j + q - j0, :T]), st[:, :T], pu[:, :T], MUL, [ru, str_], [R("hid", j + q - j0)])
                j += n
            jm = j0 + (j1 - j0 + 1) // 2
            for mm in range(8):
                c = ffn_ctr[0]
                ffn_ctr[0] += 1
                base = 4 + (c % 2) * 2
                for (ja, jb) in ((j0, jm), (jm, j1)):
                    slot, sreg = kb.wnext(("ffn_out2", li, fi, ja, jb, mm))
                    wv = slot[:, :(jb - ja) * 256].rearrange("p (j c) -> p j c", j=jb - ja)
                    for o in range(2):
                        po, ro = ps[base + o], R("ps", base + o)
                        for jj in range(ja, jb):
                            MM(po[:, :T], _r(wv[:, jj - ja, o * 128:(o + 1) * 128]), _r(hid[:, jj - j0, :T]), jj == j0, jj == j1 - 1,
                               [sreg, R("hid", jj - j0)], [ro])
                for o in range(2):
                    m = 2 * mm + o
                    po, ro = ps[base + o], R("ps", base + o)
                    STT(x[:, m, :T], po[:, :T], 0.5, x[:, m, :T], MUL, ADD, [ro, R("x", m)], [R("x", m)])

    def store_y(dst, T):
        rmsnorm(T, "fng", 0, out_r32=False)
        for tc in range(T // 128):
            for kk in range(4):
                pt, pr = ps[6 + (kk % 2)], R("ps", 6 + (kk % 2))
                for q in range(4):
                    k = kk * 4 + q
                    TR(pt[:, q * 128:(q + 1) * 128], x[:, k, tc * 128:(tc + 1) * 128], [R("x", k)], [pr])
                CP("act", xstage[:, kk * 512:(kk + 1) * 512], pt[:, :], [pr], [RXS])
            kb.dma("pool", dst[tc * 128:(tc + 1) * 128, :], xstage, reads=[RXS])

    def proj_chunk(slot, sreg, coff, T, c0, M=128):
        pp, pr = PS()
        return pp, pr

    def mixer_c(win):
        c0, T, nseg, L, kind = win["c0"], win["T"], win["nseg"], win["L"], win["kind"]
        smp = kind == "s"
        E = L + 3
        ymix = hidf[:, 0:KD * T].rearrange("p (k t) -> p k t", k=KD)
        if smp:
            o = 2048
            SCC = scr[:, o:o + 768].rearrange("p (c s k) -> p c s k", c=16, s=16)
            o += 768
            SLRU = scr[:, o:o + 256].rearrange("p (c s) -> p c s", c=16)
            o += 256
            CCO = scr[:, o:o + 768].rearrange("p (c s k) -> p c s k", c=16, s=16)
            o += 768
            LRUO = scr[:, o:o + 256].rearrange("p (c s) -> p c s", c=16)
            o += 256
            stg = scr[:, 0:D]
            kb.dma("pool", stg[0:48, :], st_convc, writes=[R("stg")])
            for ch in range(16):
                pp, pr = PS(6, 8)
                TR(pp[:, 0:48], stg[0:48, ch * 128:(ch + 1) * 128], [R("stg")], [pr])
                CP("dve", SCC[:, ch].rearrange("p s k -> p (s k)"), pp[:, 0:48], [pr], [R("SCC")])
            kb.dma("pool", stg[0:16, :], st_lru, writes=[R("stg")])
            for ch in range(16):
                pp, pr = PS(6, 8)
                TR(pp[:, 0:16], stg[0:16, ch * 128:(ch + 1) * 128], [R("stg")], [pr])
                CP("dve", SLRU[:, ch, :], pp[:, 0:16], [pr], [R("SLRU")])
        for h in range(8):
            slot, sreg = kb.wnext(("cols", "w_in_c", 0, ((D + h * 256, 256),)))
            wv = slot[:, :KD * 256].rearrange("p (k c) -> p k c", k=KD)
            xc = []
            for j in range(2):
                ch = 2 * h + j
                pp, pr = PS()
                for k in range(KD):
                    MM(pp[:, :T], _r(wv[:, k, j * 128:(j + 1) * 128]), _r(hT[:, k, c0:c0 + T]), k == 0, k == KD - 1, [sreg, R("hT", k)], [pr])
                ext, er = T_(3 + j)
                ev = ext[:, :nseg * E].rearrange("p (s e) -> p s e", s=nseg)
                if smp:
                    CP("dve", ev[:, :, 0:3], SCC[:, ch], [R("SCC")], [er])
                else:
                    CP("dve", ev[:, :, 0:3], convc_tail[:, ch:ch + 1, :], [R("convc_tail")], [er])
                CP("act", ev[:, :, 3:3 + L], v3(pp[:, :T], nseg), [pr], [er])
                if smp:
                    CP("dve", CCO[:, ch], ev[:, :, L:L + 3], [er], [R("CCO")])
                else:
                    CP("dve", convc_tail[:, ch:ch + 1, :], ev[:, :, L:L + 3], [er], [R("convc_tail")])
                acc, ar = T_(5 + j)
                a3 = v3(acc[:, :T], nseg)
                TS("dve", a3, ev[:, :, 0:L], pcol("cwc", ch), pcol("cbc", ch), MUL, ADD, [er, RP], [ar])
                for kk in (1, 2):
                    STT(a3, ev[:, :, kk:kk + L], pcol("cwc", kk * 16 + ch), a3, MUL, ADD, [er, ar, RP], [ar])
                xcj, xr = Q_(2 + j)
                STT(_r(v3(xcj[:, :T], nseg)), ev[:, :, 3:3 + L], pcol("cwc", 48 + ch), a3, MUL, ADD, [er, ar, RP], [xr])
                xc.append((xcj, xr))
            slot, sreg = kb.wnext(("gates", h))
            gv = slot[:, :1024].rearrange("p (g i c) -> p g i c", g=2, i=2)
            gts = {}
            for gi, bname in enumerate(("bga", "bgx")):
                for jc in range(2):
                    pp, pr = PS()
                    for ic in range(2):
                        MM(pp[:, :T], _r(gv[:, gi, ic, jc * 128:(jc + 1) * 128]), _r(xc[ic][0][:, :T]), ic == 0, ic == 1, [sreg, xc[ic][1]], [pr])
                    gt, gr = T_(9 + gi * 2 + jc)
                    ACT(gt[:, :T], pp[:, :T], AF.Sigmoid, [pr, RP], [gr], bias=pcol(bname, 2 * h + jc))
                    gts[(gi, jc)] = (gt, gr)
            slot, sreg = kb.wnext(("cols", "w_in_c", 0, ((h * 256, 256),)))
            wv = slot[:, :KD * 256].rearrange("p (k c) -> p k c", k=KD)
            for j in range(2):
                ch = 2 * h + j
                pp, pr = PS()
                for k in range(KD):
                    MM(pp[:, :T], _r(wv[:, k, j * 128:(j + 1) * 128]), _r(hT[:, k, c0:c0 + T]), k == 0, k == KD - 1, [sreg, R("hT", k)], [pr])
                rg, rgr = gts[(0, j)]
                ig, igr = gts[(1, j)]
                xcj, xr = xc[j]
                la, lar = T_(13)
                TS("dve", la[:, :T], rg[:, :T], dcol(0, ch), None, MUL, None, [rgr, RD], [lar])
                a, ar_ = T_(14)
                ACT(a[:, :T], la[:, :T], AF.Exp, [lar], [ar_])
                a2, a2r = T_(15)
                ACT(a2[:, :T], la[:, :T], AF.Exp, [lar], [a2r], scale=2.0)
                ACT(a2[:, :T], a2[:, :T], AF.Sqrt, [a2r], [a2r], scale=-1.0, bias=1.0)
                bt, btr = T_(16)
                TTo("dve", bt[:, :T], a2[:, :T], ig[:, :T], MUL, [a2r, igr], [btr])
                TTo("dve", bt[:, :T], bt[:, :T], xcj[:, :T], MUL, [btr, xr], [btr])
                a3, b3 = v3(a[:, :T], nseg), v3(bt[:, :T], nseg)
                t0, t0r = T_(17)
                if smp:
                    h0 = SLRU[:, ch, :].unsqueeze(2)
                    h0r = R("SLRU")
                else:
                    h0 = lru_st[:, ch:ch + 1].unsqueeze(2)
                    h0r = R("lru_st")
                tv = t0[:, :nseg].unsqueeze(2)
                TTo("dve", tv, a3[:, :, 0:1], h0, MUL, [ar_, h0r], [t0r])
                TTo("dve", b3[:, :, 0:1], b3[:, :, 0:1], tv, ADD, [btr, t0r], [btr])
                MS("dve", a3[:, :, 0:1], 0.0, [ar_])
                hh_, hr = T_(18)
                op("dve", lambda e, hh_=hh_, a=a, bt=bt: e.tensor_tensor_scan(out=hh_[:, :T], data0=a[:, :T], data1=bt[:, :T], initial=0.0,
                                                                               op0=MUL, op1=ADD), reads=[ar_, btr], writes=[hr])
                h3 = v3(hh_[:, :T], nseg)
                if smp:
                    CP("dve", LRUO[:, ch, :].unsqueeze(2), h3[:, :, L - 1:L], [hr], [R("LRUO")])
                else:
                    CP("dve", lru_st[:, ch:ch + 1].unsqueeze(2), h3[:, :, L - 1:L], [hr], [R("lru_st")])
                g2, g2r = T_(19)
                ACT(g2[:, :T], pp[:, :T], AF.Square, [pr], [g2r])
                TS("dve", g2[:, :T], g2[:, :T], 0.044715, 1.0, MUL, ADD, [g2r], [g2r])
                TTo("dve", g2[:, :T], g2[:, :T], pp[:, :T], MUL, [g2r, pr], [g2r])
                ACT(g2[:, :T], g2[:, :T], AF.Sigmoid, [g2r], [g2r], scale=1.5957691216)
                TTo("dve", g2[:, :T], g2[:, :T], pp[:, :T], MUL, [g2r, pr], [g2r])
                TTo("dve", _r(ymix[:, ch, :]), g2[:, :T], hh_[:, :T], MUL, [g2r, hr], [R("ymix", ch)])
        out_proj("w_out_c", 0, ymix, T, c0)
        if smp:
            for ch in range(16):
                pp, pr = PS(6, 8)
                TR(pp[0:16, 0:128], LRUO[:, ch, :], [R("LRUO")], [pr])
                CP("act", stg[0:16, ch * 128:(ch + 1) * 128], pp[0:16, 0:128], [pr], [R("stg")])
            kb.dma("pool", lru_s, stg[0:16, :], reads=[R("stg")])
            for ch in range(16):
                pp, pr = PS(6, 8)
                TR(pp[0:48, 0:128], CCO[:, ch].rearrange("p s k -> p (s k)"), [R("CCO")], [pr])
                CP("act", stg[0:48, ch * 128:(ch + 1) * 128], pp[0:48, 0:128], [pr], [R("stg")])
            kb.dma("pool", convc_s, stg[0:48, :], reads=[R("stg")])

    def out_proj(wname, row0, ymix, T, c0):
        for blk in range(8):
            slot, sreg = kb.wnext(("cols", wname, row0, ((blk * 256, 256),)))
            wv = slot[:, :KD * 256].rearrange("p (k c) -> p k c", k=KD)
            for j in range(2):
                m = 2 * blk + j
                pp, pr = PS()
                for k in range(KD):
                    MM(pp[:, :T], _r(wv[:, k, j * 128:(j + 1) * 128]), _r(ymix[:, k, :]), k == 0, k == KD - 1, [sreg, R("ymix", k)], [pr])
                TTo("dve", x[:, m, c0:c0 + T], x[:, m, c0:c0 + T], pp[:, :T], ADD, [pr, R("x", m)], [R("x", m)])

    def final_states_c(dst_lru, dst_conv):
        stg = scr[:, 0:D]
        pp, pr = PS(6, 8)
        TR(pp[0:16, 0:128], lru_st[:], [R("lru_st")], [pr])
        CP("act", stg[0:16, 0:128], pp[0:16, 0:128], [pr], [R("stg")])
        kb.dma("pool", dst_lru.rearrange("(k p) -> k p", p=128), stg[0:16, 0:128], reads=[R("stg")])
        for kk in range(3):
            pp, pr = PS(6, 8)
            TR(pp[0:16, 0:128], convc_tail[:, :, kk], [R("convc_tail")], [pr])
            CP("act", stg[0:16, 128 * (kk + 1):128 * (kk + 2)], pp[0:16, 0:128], [pr], [R("stg")])
            kb.dma("pool", dst_conv[kk].rearrange("(k p) -> k p", p=128), stg[0:16, 128 * (kk + 1):128 * (kk + 2)], reads=[R("stg")])

    if use_ab:
        snat = [sb("snat%d" % i, [128, 128]) for i in range(2)]
        sTt = [sb("sTt%d" % i, [128, 128]) for i in range(2)]
        cmk = [sb("cmk%d" % i, [128, 128]) for i in range(2)]
        bmk = [sb("bmk%d" % i, [128, 128]) for i in range(2)]
        snew = sb("snew", [128, 128])
        sout = [sb("sout%d" % i, [128, 128]) for i in range(2)]

    def PSI(i):
        return ps[i], R("ps", i)

    def proj128(slot, sreg, coff, tot, T, bank):
        pp, pr = PSI(bank)
        wv = slot[:, :KD * tot].rearrange("p (k c) -> p k c", k=KD)
        for k in range(KD):
            MM(pp[:, :T], _r(wv[:, k, coff:coff + 128]), _r(hT[:, k, 0:T]), k == 0, k == KD - 1, [sreg, R("hT", k)], [pr])
        return pp, pr

    def conv_a(pp, pr, cc, T, nseg, L, smp, SCA, CAO):
        E = L + 3
        ext, er = T_(8)
        ev = ext[:, :nseg * E].rearrange("p (s e) -> p s e", s=nseg)
        if smp:
            CP("dve", ev[:, :, 0:3], SCA[:, cc], [R("SCA")], [er])
        else:
            CP("dve", ev[:, :, 0:3], conva_tail[:, cc:cc + 1, :], [R("conva_tail")], [er])
        CP("act", ev[:, :, 3:3 + L], v3(pp[:, :T], nseg), [pr], [er])
        if smp:
            CP("dve", CAO[:, cc], ev[:, :, L:L + 3], [er], [R("CAO")])
        else:
            CP("dve", conva_tail[:, cc:cc + 1, :], ev[:, :, L:L + 3], [er], [R("conva_tail")])
        acc, ar = T_(9)
        a3 = v3(acc[:, :T], nseg)
        TS("dve", a3, ev[:, :, 0:L], pcol("cwa", cc), pcol("cba", cc), MUL, ADD, [er, RP], [ar])
        for kk in (1, 2, 3):
            STT(a3, ev[:, :, kk:kk + L], pcol("cwa", kk * 24 + cc), a3, MUL, ADD, [er, ar, RP], [ar])
        return acc, ar

    def mamba(win, ymix, BC, Btok, SCA, CAO):
        T, nseg, L, kind = win["T"], win["nseg"], win["L"], win["kind"]
        smp = kind == "s"
        mi = 1 if smp else 0
        nch = T // 128
        dt, dtr = T_(0)
        dta, dtar = T_(1)
        ecum, ecr = T_(2)
        tail, tlr = T_(3)
        decs = [T_(4), T_(5)]
        dtam = [T_(6), T_(7)]
        for c in range(nch):
            pp, pr = PSI(5)
            for k in range(KD):
                MM(pp[:, 0:32], _r(hT[:, k, c * 128:(c + 1) * 128]), _r(wdt[:, k, :]), k == 0, k == KD - 1, [R("hT", k), R("wdt")], [pr])
            sl = slice(c * 32, (c + 1) * 32)
            TTo("dve", dt[:, sl], pp[:, 0:32], bcast[:, 0:32], ADD, [pr, R("bcast")], [dtr])
            TS("dve", dt[:, sl], dt[:, sl], 30.0, None, MIN, None, [dtr], [dtr])
            ACT(dt[:, sl], dt[:, sl], AF.Exp, [dtr], [dtr])
            ACT(dt[:, sl], dt[:, sl], AF.Ln, [dtr], [dtr], bias=1.0)
            TTo("dve", dta[:, sl], dt[:, sl], bcast[:, 32:64], MUL, [dtr, R("bcast")], [dtar])
            pq, pqr = PSI(4)
            MM(pq[:, 0:32], m_iu[mi][:], dta[:, sl], True, True, [R("m_iu", mi), dtar], [pqr])
            MM(pq[:, 32:64], m_sl[mi][:], dta[:, sl], True, True, [R("m_sl", mi), dtar], [pqr])
            if not smp:
                MM(pq[:, 64:96], ones[:], dta[:, sl], True, True, [R("ones"), dtar], [pqr])
                ACT(decs[0][0][:, sl], pq[:, 64:96], AF.Exp, [pqr], [decs[0][1]])
            ACT(ecum[:, sl], pq[:, 0:32], AF.Exp, [pqr], [ecr])
            ACT(tail[:, sl], pq[:, 32:64], AF.Exp, [pqr], [tlr])
            if smp:
                pd, pdr = PSI(3)
                for t in range(2):
                    dm, dmr = dtam[t]
                    TTo("dve", dm[:, 0:256].rearrange("p (s h) -> p s h", s=8), dta[:, 0:32].unsqueeze(1).to_broadcast([128, 8, 32]),
                        segmask[:, 8 * t:8 * t + 8].unsqueeze(2).to_broadcast([128, 8, 32]), MUL, [dtar, R("segmask")], [dmr])
                    MM(pd[:, t * 256:(t + 1) * 256], ones[:], dm[:, 0:256], True, True, [R("ones"), dmr], [pdr])
                    ACT(decs[t][0][:, 0:256], pd[:, t * 256:(t + 1) * 256], AF.Exp, [pdr], [decs[t][1]])

        def dec_ap(c, s, h0):
            if smp:
                return decs[s // 8][0][:, (s % 8) * 32 + h0:(s % 8) * 32 + h0 + 2], decs[s // 8][1]
            return decs[0][0][:, c * 32 + h0:c * 32 + h0 + 2], decs[0][1]

        for blk in range(4):
            slot, sreg = kb.wnext(("cols", "w_in_ab", 0, ((COL_B + blk * 256, 256),)))
            for j in range(2):
                i = blk * 2 + j
                pp, pr = proj128(slot, sreg, j * 128, 256, T, 5)
                acc, ar = conv_a(pp, pr, 16 + i, T, nseg, L, smp, SCA, CAO)
                ACT(_r(BC[:, i, :]), acc[:, :T], AF.Silu, [ar], [R("BC", i)])
        cbt = {}
        for c in range(nch):
            cs = slice(c * 128, (c + 1) * 128)
            for g in range(4):
                pt, ptr = PSI(6 + g % 2)
                TR(pt[:, 0:128], BC[:, g, cs], [R("BC", g)], [ptr])
                CP("act", _r(Btok[:, c, g, :]), pt[:, 0:128], [ptr], [R("Btok")])
                pc, pcr = PSI(4)
                MM(pc[:, 0:128], _r(BC[:, g, cs]), _r(BC[:, 4 + g, cs]), True, True, [R("BC", g), R("BC", 4 + g)], [pcr])
                ti = 12 + (c * 4 + g) // 2
                ct, ctr = T_(ti)
                o = ((c * 4 + g) % 2) * 128
                CP("dve", ct[:, o:o + 128], pc[:, 0:128], [pcr], [ctr])
                cbt[(c, g)] = (ct[:, o:o + 128], ctr)
        if smp:
            for i in range(2):
                MS("dve", cmk[i][:], 0.0, [R("cmk", i)])
        ytok, ytr = T_(19)
        for hh in range(16):
            g = hh // 4
            slot, sreg = kb.wnext(("cols", "w_in_ab", 0, ((COL_XS + hh * 128, 128), (COL_Z + hh * 128, 128))))
            pp, pr = proj128(slot, sreg, 0, 256, T, 5)
            acc, ar = conv_a(pp, pr, hh, T, nseg, L, smp, SCA, CAO)
            xsh, xsr = T_(10)
            ACT(xsh[:, :T], acc[:, :T], AF.Silu, [ar], [xsr])
            pp, pr = proj128(slot, sreg, 128, 256, T, 5)
            zs, zsr = T_(11)
            ACT(zs[:, :T], pp[:, :T], AF.Silu, [pr], [zsr])
            for c in range(nch):
                cs = slice(c * 128, (c + 1) * 128)
                pt, ptr = PSI(6)
                TR(pt[:, 0:128], xsh[:, cs], [xsr], [ptr])
                xdt, xdr = Q_(2)
                xdtt, xtr = Q_(3)
                h0 = c * 32 + 2 * hh
                TTo("dve", _r(xdt[:, 0:128].rearrange("p (h q) -> p h q", h=2)), pt[:, 0:128].rearrange("p (h q) -> p h q", h=2),
                    dt[:, h0:h0 + 2].unsqueeze(2).to_broadcast([128, 2, 64]), MUL, [ptr, dtr], [xdr])
                TTo("dve", _r(xdtt[:, 0:128].rearrange("p (h q) -> p h q", h=2)), xdt[:, 0:128].rearrange("p (h q) -> p h q", h=2),
                    tail[:, h0:h0 + 2].unsqueeze(2).to_broadcast([128, 2, 64]), MUL, [xdr, tlr], [xtr])
                yd, ydr = PSI(2)
                for h2 in range(2):
                    h = h0 + h2
                    lt, ltr = T_(16)
                    TS("dve", lt[:, 0:128], m_sl[mi][:], dta[:, h:h + 1], None, MUL, None, [R("m_sl", mi), dtar], [ltr])
                    pe_, per = PSI(h2)
                    MM(pe_[:, 0:128], lt[:, 0:128], m_iu[mi][:], True, False, [ltr, R("m_iu", mi)], [per])
                    MM(pe_[:, 0:128], ident[:], m_neg[mi][:], False, True, [R("ident"), R("m_neg", mi)], [per])
                    lm, lmr = T_(17)
                    ACT(lm[:, 0:128], pe_[:, 0:128], AF.Exp, [per], [lmr])
                    G, Gr = Q_(4)
                    ct, ctr = cbt[(c, g)]
                    TTo("dve", _r(G[:, 0:128]), lm[:, 0:128], ct, MUL, [lmr, ctr], [Gr])
                    MM(yd[:, h2 * 64:(h2 + 1) * 64], _r(G[:, 0:128]), _r(xdt[:, h2 * 64:(h2 + 1) * 64]), True, True, [Gr, xdr], [ydr])
                yo, yor = PSI(3)
                hs = slice(hh * 128, (hh + 1) * 128)
                if not smp:
                    MM(yo[:, 0:128], _r(BC[:, 4 + g, cs]), _r(S_T[:, hs]), True, True, [R("BC", 4 + g), R("S_T", hh)], [yor])
                else:
                    kb.dma("pool", snat[0][:], st_ssm[0, 2 * hh:2 * hh + 2].rearrange("h p n -> (h p) n"), writes=[R("snat", 0)])
                    for s in range(NSEG):
                        nat, natr = snat[s % 2], R("snat", s % 2)
                        if s + 1 < NSEG:
                            kb.dma("pool", snat[(s + 1) % 2][:], st_ssm[s + 1, 2 * hh:2 * hh + 2].rearrange("h p n -> (h p) n"),
                                   writes=[R("snat", (s + 1) % 2)])
                        pt2, pt2r = PSI(7)
                        TR(pt2[:, 0:128], nat[:], [natr], [pt2r])
                        sT, sTr = sTt[s % 2], R("sTt", s % 2)
                        CP("act", _r(sT[:]), pt2[:, 0:128], [pt2r], [sTr])
                        cm, cmr = cmk[s % 2], R("cmk", s % 2)
                        ss = slice(s * 8, (s + 1) * 8)
                        CP("dve", _r(cm[:, ss]), BC[:, 4 + g, ss], [R("BC", 4 + g)], [cmr])
                        MM(yo[:, 0:128], _r(cm[:]), _r(sT[:]), s == 0, s == NSEG - 1, [cmr, sTr], [yor])
                        MS("dve", cm[:, ss], 0.0, [cmr])
                        bm, bmr = bmk[s % 2], R("bmk", s % 2)
                        TS("dve", _r(bm[:]), Btok[:, 0, g, :], segmask[:, s:s + 1], None, MUL, None, [R("Btok"), R("segmask")], [bmr])
                        pn, pnr = PSI(4)
                        MM(pn[:, 0:128], _r(bm[:]), _r(xdtt[:, 0:128]), True, True, [bmr, xtr], [pnr])
                        dap, dr_ = dec_ap(0, s, 2 * hh)
                        TTo("dve", snew[:].rearrange("p (h q) -> p h q", h=2), sT[:].rearrange("p (h q) -> p h q", h=2),
                            dap.unsqueeze(2).to_broadcast([128, 2, 64]), MUL, [sTr, dr_], [R("snew")])
                        TTo("dve", snew[:], snew[:], pn[:, 0:128], ADD, [R("snew"), pnr], [R("snew")])
                        pt3, pt3r = PSI(6)
                        TR(pt3[:, 0:128], snew[:], [R("snew")], [pt3r])
                        so, sor = sout[s % 2], R("sout", s % 2)
                        CP("act", so[:], pt3[:, 0:128], [pt3r], [sor])
                        kb.dma("pool", ssm_s[s, 2 * hh:2 * hh + 2].rearrange("h p n -> (h p) n"), so[:], reads=[sor])
                for h2 in range(2):
                    h = h0 + h2
                    t18, t18r = T_(18)
                    ACT(t18[:, 0:64], yo[:, h2 * 64:(h2 + 1) * 64], AF.Copy, [yor, ecr], [t18r], scale=ecum[:, h:h + 1])
                    TTo("dve", ytok[:, c * 128 + h2 * 64:c * 128 + (h2 + 1) * 64], t18[:, 0:64], yd[:, h2 * 64:(h2 + 1) * 64], ADD,
                        [t18r, ydr], [ytr])
                if not smp:
                    pn, pnr = PSI(4)
                    MM(pn[:, 0:128], _r(Btok[:, c, g, :]), _r(xdtt[:, 0:128]), True, True, [R("Btok"), xtr], [pnr])
                    dap, dr_ = dec_ap(c, 0, 2 * hh)
                    TTo("dve", _r(S_T[:, hs].rearrange("p (h q) -> p h q", h=2)), S_T[:, hs].rearrange("p (h q) -> p h q", h=2),
                        dap.unsqueeze(2).to_broadcast([128, 2, 64]), MUL, [R("S_T", hh), dr_], [R("S_T", hh)])
                    TTo("dve", _r(S_T[:, hs]), S_T[:, hs], pn[:, 0:128], ADD, [R("S_T", hh), pnr], [R("S_T", hh)])
            for c in range(nch):
                cs = slice(c * 128, (c + 1) * 128)
                pt, ptr = PSI(7)
                TR(pt[:, 0:128], ytok[:, cs], [ytr], [ptr])
                t18, t18r = T_(18)
                STT(t18[:, 0:128], xsh[:, cs], dcol(48, hh), pt[:, 0:128], MUL, ADD, [xsr, RD, ptr], [t18r])
                TTo("dve", _r(ymix[:, hh, cs]), t18[:, 0:128], zs[:, cs], MUL, [t18r, zsr], [R("ymix", hh)])
            if hh % 4 == 3:
                for q in range(4):
                    sqt, sqr = Q_(q % 2)
                    ACT(_r(sqt[:, :T]), ymix[:, 4 * g + q, :], AF.Square, [R("ymix", 4 * g + q)], [sqr])
                    MM(ps[5][:, :T], _r(ones[:]), _r(sqt[:, :T]), q == 0, q == 3, [sqr, R("ones")], [R("ps", 5)])
                rs_, rsr = T_(18)
                ACT(rs_[:, :T], ps[5][:, :T], AF.Sqrt, [R("ps", 5)], [rsr], scale=1.0 / 512, bias=EPS)
                op("dve", lambda e, rs_=rs_: e.reciprocal(out=rs_[:, :T], in_=rs_[:, :T]), reads=[rsr], writes=[rsr])
                for q in range(4):
                    ch = 4 * g + q
                    STT(_r(ymix[:, ch, :]), ymix[:, ch, :], pcol("gna", ch), rs_[:, :T], MUL, MUL, [R("ymix", ch), rsr, RP], [R("ymix", ch)])

    def mixer_ab(win):
        T, nseg, L, kind = win["T"], win["nseg"], win["L"], win["kind"]
        smp = kind == "s"
        nch = T // 128
        ymix = hidf[:, 0:KD * T].rearrange("p (k t) -> p k t", k=KD)
        BC = hidf[:, 4096:4096 + 8 * T].rearrange("p (k t) -> p k t", k=8)
        Btok = hidf[:, 6144:6144 + nch * 512].rearrange("p (c g n) -> p c g n", c=nch, g=4)
        SCA = CAO = SSH = SHO = None
        stg = scr[:, 0:D]
        if smp:
            SCA = scr[:, 2048:3200].rearrange("p (c s k) -> p c s k", c=24, s=16)
            CAO = scr[:, 3200:4352].rearrange("p (c s k) -> p c s k", c=24, s=16)
            SSH = scr[:, 4352:5152].rearrange("p (c s) -> p c s", c=50)
            SHO = scr[:, 5152:5952].rearrange("p (c s) -> p c s", c=50)
            for (c0_, n_) in ((0, 16), (16, 8)):
                kb.dma("pool", stg[0:48, 0:n_ * 128], st_conva[:, c0_ * 128:(c0_ + n_) * 128], writes=[R("stg")])
                for ch in range(n_):
                    pp, pr = PSI(6 + ch % 2)
                    TR(pp[:, 0:48], stg[0:48, ch * 128:(ch + 1) * 128], [R("stg")], [pr])
                    CP("dve", SCA[:, c0_ + ch].rearrange("p s k -> p (s k)"), pp[:, 0:48], [pr], [R("SCA")])
        if "a" in cfg.get("ab_parts", "ab"):
            mamba(win, ymix, BC, Btok, SCA, CAO)
            out_proj("w_out_ab", 0, ymix, T, 0)
            kb.barrier()
        if "b" in cfg.get("ab_parts", "ab"):
            rwkv(win, ymix, SSH, SHO)
            out_proj("w_out_ab", D, ymix, T, 0)
            kb.barrier()
        if smp:
            if "a" in cfg.get("ab_parts", "ab"):
                for (c0_, n_) in ((0, 16), (16, 8)):
                    for ch in range(n_):
                        pp, pr = PSI(6 + ch % 2)
                        TR(pp[0:48, 0:128], CAO[:, c0_ + ch].rearrange("p s k -> p (s k)"), [R("CAO")], [pr])
                        CP("act", stg[0:48, ch * 128:(ch + 1) * 128], pp[0:48, 0:128], [pr], [R("stg")])
                    kb.dma("pool", conva_s[:, c0_ * 128:(c0_ + n_) * 128], stg[0:48, 0:n_ * 128], reads=[R("stg")])

    def final_states_ab():
        stg = scr[:, 0:D]
        if "a" in cfg.get("ab_parts", "ab"):
            for hh in range(16):
                pt, ptr = PSI(6 + hh % 2)
                TR(pt[:, 0:128], S_T[:, hh * 128:(hh + 1) * 128], [R("S_T", hh)], [ptr])
                so, sor = sout[hh % 2], R("sout", hh % 2)
                CP("act", so[:], pt[:, 0:128], [ptr], [sor])
                kb.dma("pool", ssm_p[2 * hh:2 * hh + 2].rearrange("h p n -> (h p) n"), so[:], reads=[sor])
            for kk in range(3):
                pp, pr = PSI(6 + kk % 2)
                TR(pp[0:24, 0:128], conva_tail[:, :, kk], [R("conva_tail")], [pr])
                CP("act", stg[0:24, 128 * kk:128 * (kk + 1)], pp[0:24, 0:128], [pr], [R("stg")])
                kb.dma("pool", conva_p[kk].rearrange("(k p) -> k p", p=128), stg[0:24, 128 * kk:128 * (kk + 1)], reads=[R("stg")])
        if "b" in cfg.get("ab_parts", "ab"):
            final_states_b()

    def shift_chunk(pp, pr, ci, T, nseg, L, smp, SSH, SHO, out, outr, r32out=False, func=None, rows=None):
        E = L + 1
        ext, er = T_(0)
        ev = ext[:, :nseg * E].rearrange("p (s e) -> p s e", s=nseg)
        if smp:
            CP("dve", ev[:, :, 0:1], SSH[:, ci, :].unsqueeze(2), [R("SSH")], [er])
        else:
            CP("dve", ev[:, :, 0:1], shift_tail[:, ci:ci + 1].unsqueeze(2), [R("shift_tail")], [er])
        CP("act", ev[:, :, 1:1 + L], v3(pp[:, :T], nseg), [pr], [er])
        if smp:
            CP("dve", SHO[:, ci, :].unsqueeze(2), ev[:, :, L:L + 1], [er], [R("SHO")])
        else:
            CP("dve", shift_tail[:, ci:ci + 1].unsqueeze(2), ev[:, :, L:L + 1], [er], [R("shift_tail")])
        d, dr = T_(12)
        d3 = v3(d[:, :T], nseg)
        TTo("dve", d3, ev[:, :, 0:L], ev[:, :, 1:1 + L], SUB, [er], [dr])
        STT(v3(out[:, :T], nseg), d3, pcol("mu", ci), ev[:, :, 1:1 + L], MUL, ADD, [dr, er, RP], [outr])

    def rwkv(win, ymix, SSH, SHO):
        T, nseg, L, kind = win["T"], win["nseg"], win["L"], win["kind"]
        smp = kind == "s"
        mi = 1 if smp else 0
        nch = T // 128
        nit = 2 if smp else 6
        stg = scr[:, 0:D]
        if smp:
            for b in range(4):
                nb = min(16, 50 - 16 * b)
                kb.dma("pool", stg[0:16, 0:nb * 128], st_shift[:, b * 2048:b * 2048 + nb * 128], writes=[R("stg")])
                for ch in range(nb):
                    pp, pr = PSI(6 + ch % 2)
                    TR(pp[:, 0:16], stg[0:16, ch * 128:(ch + 1) * 128], [R("stg")], [pr])
                    CP("dve", SSH[:, 16 * b + ch, :], pp[:, 0:16], [pr], [R("SSH")])
        TX, TXr = Q_(2)
        SG, SGr = Q_(3)
        slot, sreg = kb.wnext(("cols", "w_in_ab", 0, ((COL_LW, 256),)))
        pp, pr = proj128(slot, sreg, 0, 256, T, 5)
        t1, t1r = T_(1)
        shift_chunk(pp, pr, 48, T, nseg, L, smp, SSH, SHO, t1, t1r)
        ACT(_r(TX[0:64, :T]), t1[0:64, :T], AF.Tanh, [t1r], [TXr])
        CP("dve", _r(TX[64:128, :T]), t1[64:128, :T], [t1r], [TXr])
        pp, pr = proj128(slot, sreg, 128, 256, T, 5)
        shift_chunk(pp, pr, 49, T, nseg, L, smp, SSH, SHO, t1, t1r)
        ACT(_r(SG[:, :T]), t1[:, :T], AF.Sigmoid, [t1r], [SGr])
        otok, otr = T_(15)
        for hh in range(16):
            slotA, srA = kb.wnext(("cols", "w_in_ab", 0, ((COL_R + hh * 128, 128), (COL_K + hh * 128, 128))))
            rs, rsr = T_(1)
            ks, ksr = T_(2)
            vs, vsr = T_(3)
            pp, pr = proj128(slotA, srA, 0, 256, T, 5)
            shift_chunk(pp, pr, hh, T, nseg, L, smp, SSH, SHO, rs, rsr)
            pp, pr = proj128(slotA, srA, 128, 256, T, 5)
            shift_chunk(pp, pr, 16 + hh, T, nseg, L, smp, SSH, SHO, ks, ksr)
            slotB, srB = kb.wnext(("lora", hh))
            pp, pr = proj128(slotB, srB, 0, 128, T, 5)
            shift_chunk(pp, pr, 32 + hh, T, nseg, L, smp, SSH, SHO, vs, vsr)
            lw = 2048
            pd, pdr = PSI(4)
            MM(pd[:, :T], _r(slotB[0:64, lw:lw + 128]), _r(TX[0:64, :T]), True, True, [srB, TXr], [pdr])
            ew, ewr = T_(4)
            ACT(ew[:, :T], pd[:, :T], AF.Exp, [pdr, RD], [ewr], scale=-1.0, bias=dcol(16, hh))
            ACT(ew[:, :T], ew[:, :T], AF.Ln, [ewr], [ewr], bias=1.0)
            ACT(ew[:, :T], ew[:, :T], AF.Exp, [ewr], [ewr], scale=-1.0, bias=-0.5)
            cg, cgr = T_(5)
            op("dve", lambda e, cg=cg, ew=ew: e.tensor_tensor_scan(out=cg[:, :T], data0=m_reset[mi][:, :T], data1=ew[:, :T], initial=0.0,
                                                                     op0=MUL, op1=ADD), reads=[ewr, R("m_reset", mi)], writes=[cgr])
            gam, gmr = T_(6)
            ig, igr = T_(7)
            gp, gpr = T_(8)
            ACT(gam[:, :T], cg[:, :T], AF.Exp, [cgr], [gmr], scale=-1.0)
            ACT(ig[:, :T], cg[:, :T], AF.Exp, [cgr], [igr])
            TTo("dve", gp[:, :T], ew[:, :T], cg[:, :T], SUB, [ewr, cgr], [gpr])
            ACT(gp[:, :T], gp[:, :T], AF.Exp, [gpr], [gpr])
            pa, par = PSI(4)
            MM(pa[:, :T], _r(slotB[64:128, lw:lw + 128]), _r(TX[64:128, :T]), True, True, [srB, TXr], [par])
            al, alr = T_(9)
            ACT(al[:, :T], pa[:, :T], AF.Sigmoid, [par, RP], [alr], bias=pcol("a0", hh))
            kkn, kkr = T_(10)
            TS("dve", kkn[:, :T], ks[:, :T], pcol("kk", hh), None, MUL, None, [ksr, RP], [kkr])
            sq, sqr = Q_(0)
            ACT(_r(sq[:, :T]), kkn[:, :T], AF.Square, [kkr], [sqr])
            pn_, pnr = PSI(4)
            MM(pn_[:, :T], _r(bones[:]), _r(sq[:, :T]), True, True, [R("bones"), sqr], [pnr])
            t12, t12r = T_(12)
            ACT(t12[:, :T], pn_[:, :T], AF.Sqrt, [pnr], [t12r])
            TS("dve", t12[:, :T], t12[:, :T], 1e-12, None, MAX, None, [t12r], [t12r])
            op("dve", lambda e, t12=t12: e.reciprocal(out=t12[:, :T], in_=t12[:, :T]), reads=[t12r], writes=[t12r])
            TTo("dve", kkn[:, :T], kkn[:, :T], t12[:, :T], MUL, [kkr, t12r], [kkr])
            kp, kpr = T_(11)
            TS("dve", kp[:, :T], al[:, :T], pcol("ka", hh), dcol(32, hh), MUL, ADD, [alr, RP, RD], [kpr])
            TTo("dve", kp[:, :T], kp[:, :T], ks[:, :T], MUL, [kpr, ksr], [kpr])
            AR = [Q_(4 + c) for c in range(nch)]
            for c in range(nch):
                cs = slice(c * 128, (c + 1) * 128)
                TTo("dve", _r(AR[c][0][:, 0:128]), kkn[:, cs], gp[:, cs], MUL, [kkr, gpr], [AR[c][1]])
                TTo("dve", _r(AR[c][0][:, 128:256]), rs[:, cs], gam[:, cs], MUL, [rsr, gmr], [AR[c][1]])
            BT, BTr = Q_(6)
            KT, KTr = Q_(7)
            TTo("dve", t12[:, :T], kkn[:, :T], al[:, :T], MUL, [kkr, alr], [t12r])
            STT(_r(BT[:, :T]), t12[:, :T], -1.0, ig[:, :T], MUL, MUL, [t12r, igr], [BTr])
            TTo("dve", _r(KT[:, :T]), kp[:, :T], ig[:, :T], MUL, [kpr, igr], [KTr])
            TTo("dve", t12[:, :T], rs[:, :T], kp[:, :T], MUL, [rsr, kpr], [t12r])
            sq1, sq1r = Q_(1)
            TS("dve", _r(sq1[:, :T]), t12[:, :T], pcol("rk", hh), None, MUL, None, [t12r, RP], [sq1r])
            pb, pbr = PSI(4)
            MM(pb[:, :T], _r(bones[:]), _r(sq1[:, :T]), True, True, [R("bones"), sq1r], [pbr])
            bon, bonr = T_(13)
            TTo("dve", bon[:, :T], vs[:, :T], pb[:, :T], MUL, [vsr, pbr], [bonr])
            pg, pgr = PSI(4)
            MM(pg[:, :T], _r(slotB[:, lw + 128:lw + 256]), _r(SG[:, :T]), True, True, [srB, SGr], [pgr])
            gg, ggr = T_(14)
            CP("act", gg[:, :T], pg[:, :T], [pgr], [ggr])
            Vtok, Vtr = Q_(8)
            Btk, Btkr = Q_(9)
            Ktk, Ktkr = Q_(10)
            for c in range(nch):
                cs = slice(c * 128, (c + 1) * 128)
                for (src, srcr, dst, dstr, bank) in ((vs, vsr, Vtok, Vtr, 6), (BT, BTr, Btk, Btkr, 7), (KT, KTr, Ktk, Ktkr, 6)):
                    pt, ptr = PSI(bank)
                    TR(pt[:, 0:128], src[:, cs], [srcr], [ptr])
                    CP("act", _r(dst[:, cs]), pt[:, 0:128], [ptr], [dstr])
            if smp:
                for s in range(NSEG):
                    nat, natr = snat[s % 2], R("snat", s % 2)
                    kb.dma("pool", nat[0:64, :].rearrange("p (h k) -> p h k", h=2), st_wkv[s, 2 * hh:2 * hh + 2].rearrange("h v k -> v h k"),
                           writes=[natr])
                    pt, ptr = PSI(6 + s % 2)
                    TR(pt[:, 0:64], nat[0:64, :], [natr], [ptr])
                    CP("act", _r(S0T[:, s, :]), pt[:, 0:64], [ptr], [R("S0T", s)])
                segs = [(s, slice(s * 8, (s + 1) * 8), s * 8 + 7) for s in range(NSEG)]
            for c in range(nch):
                cs = slice(c * 128, (c + 1) * 128)
                if not smp:
                    segs = [(hh, slice(0, 128), 127)]
                ARc, ARr = AR[c]
                NA = [Q_(11), Q_(12)]
                KA = [Q_(13), Q_(14)]
                Pm = [Q_(15), Q_(16)]
                Xm = [Q_(17), Q_(18)]
                Zs, Zsr = Q_(19)
                Up, Upr = Q_(20)
                for h2 in range(2):
                    rows = slice(h2 * 64, h2 * 64 + 64)
                    bp, bpr = PSI(h2 * 2)
                    bx, bxr = PSI(h2 * 2 + 1)
                    MM(bp[:, 0:128], _r(ARc[rows, 0:128]), _r(BT[rows, cs]), True, True, [ARr, BTr], [bpr])
                    TTo("dve", _r(Pm[h2][0][:, 0:128]), bp[:, 0:128], m_sl[mi][:], MUL, [bpr, R("m_sl", mi)], [Pm[h2][1]])
                    MM(bx[:, 0:256], _r(BT[rows, cs]), _r(ARc[rows, 0:256]), True, True, [ARr, BTr], [bxr])
                    TTo("dve", _r(NA[h2][0][:, 0:256]), bx[:, 0:256], m_pair[mi][:], MUL, [bxr, R("m_pair", mi)], [NA[h2][1]])
                    MM(bx[:, 0:256], _r(KT[rows, cs]), _r(ARc[rows, 0:256]), True, True, [ARr, KTr], [bxr])
                    TTo("dve", _r(KA[h2][0][:, 0:256]), bx[:, 0:256], m_pair[mi][:], MUL, [bxr, R("m_pair", mi)], [KA[h2][1]])
                    CP("act", _r(Xm[h2][0][:, 0:128]), NA[h2][0][:, 0:128], [NA[h2][1]], [R("XP", h2)])
                    TTo("dve", _r(Xm[h2][0][:, 128:256]), NA[h2][0][:, 0:128], ident[:], ADD, [NA[h2][1], R("ident")], [R("XW", h2)])
                for it in range(nit):
                    for h2 in range(2):
                        bp, bpr = PSI(h2 * 2)
                        bx, bxr = PSI(h2 * 2 + 1)
                        P_, Pr_ = Pm[h2]
                        X_ = Xm[h2][0]
                        last = it == nit - 1
                        MM(bp[:, 0:128], _r(X_[:, 0:128]), _r(P_[:, 0:128]), True, True, [R("XP", h2), Pr_], [bpr])
                        if not last:
                            MM(bx[:, 0:128], _r(P_[:, 0:128]), _r(X_[:, 0:128]), True, True, [R("XP", h2), Pr_], [bxr])
                        CP("act", _r(P_[:, 0:128]), bp[:, 0:128], [bpr], [Pr_])
                        if not last:
                            CP("act", _r(X_[:, 0:128]), bx[:, 0:128], [bxr], [R("XP", h2)])
                        MM(bx[:, 128:256], _r(P_[:, 0:128]), _r(X_[:, 128:256]), True, True, [Pr_, R("XW", h2)], [bxr])
                        TTo("dve", _r(X_[:, 128:256]), X_[:, 128:256], bx[:, 128:256], ADD, [R("XW", h2), bxr], [R("XW", h2)])
                for h2 in range(2):
                    rows = slice(h2 * 64, h2 * 64 + 64)
                    vcols = slice(c * 128 + h2 * 64, c * 128 + h2 * 64 + 64)
                    b4, b4r = PSI(4)
                    MM(b4[0:64, 0:128], _r(Vtok[:, vcols]), _r(KA[h2][0][:, 0:128]), True, False, [Vtr, KA[h2][1]], [b4r])
                    for si, (idx, sc, endc) in enumerate(segs):
                        MM(b4[0:64, sc], _r(S0T[rows, idx, :]), _r(ARc[rows, sc]), False, si == len(segs) - 1, [R("S0T", idx), ARr], [b4r])
                    zt, ztr = T_(18)
                    CP("act", zt[0:64, 0:128], b4[0:64, 0:128], [b4r], [ztr])
                    pt, ptr = PSI(6)
                    TR(pt[:, 0:64], zt[0:64, 0:128], [ztr], [ptr])
                    CP("act", _r(Zs[:, h2 * 64:(h2 + 1) * 64]), pt[:, 0:64], [ptr], [Zsr])
                    b5, b5r = PSI(5)
                    MM(b5[:, 0:64], _r(Xm[h2][0][:, 128:256]), _r(Zs[:, h2 * 64:(h2 + 1) * 64]), True, True, [R("XW", h2), Zsr], [b5r])
                    CP("act", _r(Up[:, h2 * 64:(h2 + 1) * 64]), b5[:, 0:64], [b5r], [Upr])
                    b4, b4r = PSI(4)
                    MM(b4[0:64, 0:128], _r(Up[:, h2 * 64:(h2 + 1) * 64]), _r(NA[h2][0][:, 128:256]), True, False, [Upr, NA[h2][1]], [b4r])
                    MM(b4[0:64, 0:128], _r(Vtok[:, vcols]), _r(KA[h2][0][:, 128:256]), False, False, [Vtr, KA[h2][1]], [b4r])
                    for si, (idx, sc, endc) in enumerate(segs):
                        sc2 = slice(128 + sc.start, 128 + sc.stop)
                        MM(b4[0:64, sc], _r(S0T[rows, idx, :]), _r(ARc[rows, sc2]), False, si == len(segs) - 1, [R("S0T", idx), ARr], [b4r])
                    CP("act", zt[0:64, 0:128], b4[0:64, 0:128], [b4r], [ztr])
                    pt, ptr = PSI(7)
                    TR(pt[:, 0:64], zt[0:64, 0:128], [ztr], [ptr])
                    CP("act", otok[:, vcols], pt[:, 0:64], [ptr], [otr])
                for (idx, sc, endc) in segs:
                    if smp:
                        bm, bmr = bmk[0], R("bmk", 0)
                        km, kmr = bmk[1], R("bmk", 1)
                        TS("dve", _r(bm[:]), Btk[:, cs], segmask[:, idx:idx + 1], None, MUL, None, [Btkr, R("segmask")], [bmr])
                        TS("dve", _r(km[:]), Ktk[:, cs], segmask[:, idx:idx + 1], None, MUL, None, [Ktkr, R("segmask")], [kmr])
                        bl, blr, kl, klr = bm[:], bmr, km[:], kmr
                    else:
                        bl, blr, kl, klr = Btk[:, cs], Btkr, Ktk[:, cs], Ktkr
                    b5, b5r = PSI(5)
                    MM(b5[:, 0:128], _r(bl), _r(Up[:, 0:128]), True, False, [blr, Upr], [b5r])
                    MM(b5[:, 0:128], _r(kl), _r(Vtok[:, cs]), False, True, [klr, Vtr], [b5r])
                    ec = c * 128 + endc
                    for h2 in range(2):
                        rows = slice(h2 * 64, h2 * 64 + 64)
                        t16, t16r = T_(16)
                        TTo("dve", t16[rows, 0:64], S0T[rows, idx, :], b5[rows, h2 * 64:(h2 + 1) * 64], ADD, [R("S0T", idx), b5r], [t16r])
                        TS("dve", _r(S0T[rows, idx, :]), t16[rows, 0:64], gam[rows, ec:ec + 1], None, MUL, None, [t16r, gmr], [R("S0T", idx)])
                o3 = otok[:, cs].rearrange("p (h q) -> p h q", h=2)
                st_, str_ = T_(17)
                op("dve", lambda e, o3=o3, st_=st_: e.tensor_reduce(out=st_[:, 0:2], in_=o3, axis=AX.X, op=ADD), reads=[otr], writes=[str_])
                TS("dve", st_[:, 0:2], st_[:, 0:2], 1.0 / 64, None, MUL, None, [str_], [str_])
                cen, cenr = T_(16)
                c3 = cen[:, 0:128].rearrange("p (h q) -> p h q", h=2)
                TTo("dve", c3, o3, st_[:, 0:2].unsqueeze(2).to_broadcast([128, 2, 64]), SUB, [otr, str_], [cenr])
                t19, t19r = T_(19)
                TTo("dve", t19[:, 0:128], cen[:, 0:128], cen[:, 0:128], MUL, [cenr], [t19r])
                op("dve", lambda e, t19=t19, st_=st_: e.tensor_reduce(out=st_[:, 2:4], in_=t19[:, 0:128].rearrange("p (h q) -> p h q", h=2),
                                                                       axis=AX.X, op=ADD), reads=[t19r], writes=[str_])
                ACT(st_[:, 2:4], st_[:, 2:4], AF.Sqrt, [str_], [str_], scale=1.0 / 64, bias=GN_EPS)
                op("dve", lambda e, st_=st_: e.reciprocal(out=st_[:, 2:4], in_=st_[:, 2:4]), reads=[str_], writes=[str_])
                TTo("dve", c3, c3, st_[:, 2:4].unsqueeze(2).to_broadcast([128, 2, 64]), MUL, [cenr, str_], [cenr])
                pt, ptr = PSI(6)
                TR(pt[:, 0:128], cen[:, 0:128], [cenr], [ptr])
                ACT(t19[:, 0:128], pt[:, 0:128], AF.Identity, [ptr, RP], [t19r], scale=pcol("lnw", hh), bias=pcol("lnb", hh))
                TTo("dve", t19[:, 0:128], t19[:, 0:128], bon[:, cs], ADD, [t19r, bonr], [t19r])
                TTo("dve", _r(ymix[:, hh, cs]), t19[:, 0:128], gg[:, cs], MUL, [t19r, ggr], [R("ymix", hh)])
            if smp:
                for s in range(NSEG):
                    pt, ptr = PSI(6 + s % 2)
                    TR(pt[0:64, 0:128], S0T[:, s, :], [R("S0T", s)], [ptr])
                    so, sor = sout[s % 2], R("sout", s % 2)
                    CP("act", so[0:64, :], pt[0:64, 0:128], [ptr], [sor])
                    kb.dma("pool", wkv_s[s, 2 * hh:2 * hh + 2].rearrange("h v k -> v h k"), so[0:64, :].rearrange("p (h k) -> p h k", h=2),
                           reads=[sor])
        if smp:
            for b in range(4):
                nb = min(16, 50 - 16 * b)
                for ch in range(nb):
                    pp, pr = PSI(6 + ch % 2)
                    TR(pp[0:16, 0:128], SHO[:, 16 * b + ch, :], [R("SHO")], [pr])
                    CP("act", stg[0:16, ch * 128:(ch + 1) * 128], pp[0:16, 0:128], [pr], [R("stg")])
                kb.dma("pool", shift_s[:, b * 2048:b * 2048 + nb * 128], stg[0:16, 0:nb * 128], reads=[R("stg")])

    def final_states_b():
        stg = scr[:, 0:D]
        for hh in range(16):
            pt, ptr = PSI(6 + hh % 2)
            TR(pt[0:64, 0:128], S0T[:, hh, :], [R("S0T", hh)], [ptr])
            so, sor = sout[hh % 2], R("sout", hh % 2)
            CP("act", so[0:64, :], pt[0:64, 0:128], [ptr], [sor])
            kb.dma("pool", wkv_p[2 * hh:2 * hh + 2].rearrange("h v k -> v h k"), so[0:64, :].rearrange("p (h k) -> p h k", h=2), reads=[sor])
        pp, pr = PSI(6)
        TR(pp[0:50, 0:128], shift_tail[:], [R("shift_tail")], [pr])
        CP("act", stg[0:50, 0:128], pp[0:50, 0:128], [pr], [R("stg")])
        kb.dma("pool", shift_p.rearrange("(k p) -> k p", p=128), stg[0:50, 0:128], reads=[R("stg")])

    tiles = cfg["tiles"]
    for ti, tile in enumerate(tiles):
        if tile[0] == "p":
            t0 = tile[1]
            T = 256
            load_x(xp[t0:t0 + T, :], T)
            dst = yp[t0:t0 + T, :]
            wins = [dict(c0=0, T=256, nseg=1, L=256, kind="p")]
        else:
            T = 128
            load_x(xs, T)
            dst = ys
            wins = [dict(c0=0, T=128, nseg=16, L=8, kind="s")]
        for st in stages:
            kb.barrier()
            if st.startswith("ffn"):
                ffn(T, int(st[3]), int(st[4]))
            elif st == "mixc":
                rmsnorm(T, "ng", (1 * 3 + 1) * KD)
                for w in wins:
                    mixer_c(w)
                    kb.barrier()
            elif st == "mixab":
                rmsnorm(T, "ng", (0 * 3 + 1) * KD)
                for w in wins:
                    mixer_ab(w)
                    kb.barrier()
        kb.barrier()
        store_y(dst, T)
        last_prompt = tile[0] == "p" and (ti + 1 == len(tiles) or tiles[ti + 1][0] != "p")
        if last_prompt:
            kb.barrier()
            if use_c:
                final_states_c(lru_p, convc_p)
            if use_ab:
                final_states_ab()

    kb.finish()


ALL_STAGES = ["ffn00", "mixab", "ffn01", "ffn10", "mixc", "ffn11"]
FULL_CFG = {"tiles": [("p", 256 * i) for i in range(8)] + [("s",)], "stages": ALL_STAGES}


def make_in_maps(inp, cfg):
    f = lambda a: np.ascontiguousarray(np.asarray(a, dtype=np.float32))
    stages = cfg["stages"]
    use_ffn = any(s.startswith("ffn") for s in stages)
    use_ab = "mixab" in stages
    use_c = "mixc" in stages
    shared = {"norm_gain": f(inp["norm_gain"]), "final_norm_gain": f(inp["final_norm_gain"])}
    if use_ffn:
        shared["w_ffn_in"] = f(inp["w_ffn_in"])
        shared["w_ffn_out"] = f(inp["w_ffn_out"])
    if use_c:
        for k in ("w_in_c", "w_out_c", "conv_w_c", "conv_b_c", "w_gate_a_c", "w_gate_x_c", "b_gate_a_c", "b_gate_x_c", "lambda_c"):
            shared[k] = f(inp[k][0])
    if use_ab:
        for k in ("w_in_ab", "w_out_ab", "conv_w_a", "conv_b_a", "dt_bias_a", "a_log_a", "d_skip_a", "gnorm_a", "mu_b", "w0_b", "w2_b",
                  "a0_b", "a2_b", "g2_b", "k_k_b", "k_a_b", "ln_w_b", "ln_b_b"):
            shared[k] = f(inp[k][0])
        shared["r_k_b"] = f(inp["r_k_b"][0]).reshape(D)
    in_maps = []
    for c in range(8):
        m = dict(shared)
        m["xp"] = f(inp["x_prompt"][c % 4])
        sl = slice(16 * c, 16 * c + 16)
        m["xs"] = f(inp["x_sample"][sl]).reshape(128, D)
        if use_c:
            m["st_lru"] = f(inp["state_lru_c"][0, sl])
            m["st_convc"] = f(inp["state_conv_c"][0, sl]).reshape(48, D)
        if use_ab:
            m["st_ssm"] = f(inp["state_ssm_a"][0, sl])
            m["st_conva"] = f(inp["state_conv_a"][0, sl]).reshape(48, 3072)
            m["st_wkv"] = f(inp["state_wkv_b"][0, sl])
            m["st_shift"] = f(inp["state_shift_b"][0, sl]).reshape(16, 6400)
        in_maps.append(m)
    return in_maps


def gather(r, cfg):
    stages = cfg["stages"]
    use_ab = "mixab" in stages
    use_c = "mixc" in stages
    out = {}
    out["y_prompt"] = np.stack([r[c]["yp"] for c in range(4)], 0)
    out["y_sample"] = np.concatenate([r[c]["ys"].reshape(16, 8, D) for c in range(8)], 0)
    if use_ab:
        out["ssm_p"] = np.stack([r[c]["ssm_p"] for c in range(4)], 0)[None]
        out["conva_p"] = np.stack([r[c]["conva_p"] for c in range(4)], 0)[None]
        out["wkv_p"] = np.stack([r[c]["wkv_p"] for c in range(4)], 0)[None]
        out["shift_p"] = np.stack([r[c]["shift_p"].reshape(1, 6400) for c in range(4)], 0)[None]
        out["ssm_s"] = np.concatenate([r[c]["ssm_s"] for c in range(8)], 0)[None]
        out["conva_s"] = np.concatenate([r[c]["conva_s"].reshape(16, 3, 3072) for c in range(8)], 0)[None]
        out["wkv_s"] = np.concatenate([r[c]["wkv_s"] for c in range(8)], 0)[None]
        out["shift_s"] = np.concatenate([r[c]["shift_s"].reshape(16, 1, 6400) for c in range(8)], 0)[None]
    if use_c:
        out["lru_p"] = np.stack([r[c]["lru_p"] for c in range(4)], 0)[None]
        out["convc_p"] = np.stack([r[c]["convc_p"] for c in range(4)], 0)[None]
        out["lru_s"] = np.concatenate([r[c]["lru_s"] for c in range(8)], 0)[None]
        out["convc_s"] = np.concatenate([r[c]["convc_s"].reshape(16, 3, D) for c in range(8)], 0)[None]
    return out


def kernel(**inp):
    cfg = FULL_CFG
    nc, kb = build(cfg)
    in_maps = make_in_maps(inp, cfg)
    res = run_bass_kernel_spmd(nc, in_maps, core_ids=list(range(8)))
    o = gather(res.results, cfg)
    return (o["y_prompt"], o["y_sample"], o["ssm_p"], o["conva_p"], o["wkv_p"], o["shift_p"], o["lru_p"], o["convc_p"],
            o["ssm_s"], o["conva_s"], o["wkv_s"], o["shift_s"], o["lru_s"], o["convc_s"])
```
